# Optimizing a Trainium2 kernel written in Bass

```python
import math
import jax, jax.numpy as jnp
from jax import lax
import numpy as np

D_MODEL = 1024
BATCH = 1
SEQ = 16384
DEPTH = 1

SSD_EXPAND = 2
SSD_INNER = SSD_EXPAND * D_MODEL
SSD_HEAD_DIM = 64
SSD_HEADS = SSD_INNER // SSD_HEAD_DIM
SSD_GROUPS = 4
SSD_HEADS_PER_GROUP = SSD_HEADS // SSD_GROUPS
SSD_STATE = 128
SSD_CONV = 4
SSD_CHUNK = 128
SSD_CONV_DIM = SSD_INNER + 2 * SSD_GROUPS * SSD_STATE

POOL_WINDOWS = (2, 4, 8, 16)
POOL_GROUPS = len(POOL_WINDOWS)
POOL_WIDTH = D_MODEL
POOL_GROUP_DIM = POOL_WIDTH // POOL_GROUPS

FFN_HIDDEN = 2816
FFN_CONV = 3

N_BRANCHES = 2
N_MOD = 6
EPS = 1e-6

IN_WIDTH = SSD_INNER + SSD_CONV_DIM + SSD_HEADS + POOL_WIDTH + N_BRANCHES * D_MODEL

kernel_name = "hybrid_ssd_pool_convglu_adaln_block"


def rmsnorm(x, g):
    xf = x.astype(jnp.float32)
    y = xf * lax.rsqrt(jnp.mean(xf * xf, axis=-1, keepdims=True) + EPS)
    return (y * g.astype(jnp.float32)).astype(x.dtype)


def group_rmsnorm(x, g, n_groups):
    b, s, d = x.shape
    xf = x.astype(jnp.float32).reshape(b, s, n_groups, d // n_groups)
    y = xf * lax.rsqrt(jnp.mean(xf * xf, axis=-1, keepdims=True) + EPS)
    return (y.reshape(b, s, d) * g.astype(jnp.float32)).astype(x.dtype)


def causal_dwconv(u, w, b):
    k_width = w.shape[0]
    s = u.shape[1]
    up = jnp.pad(u, ((0, 0), (k_width - 1, 0), (0, 0)))
    y = b + w[0] * up[:, 0:s]
    for k in range(1, k_width):
        y = y + w[k] * up[:, k:k + s]
    return y


def ssd_chunked(xh, dt, a, bm, cm):
    bsz, s, _, p = xh.shape
    nc = s // SSD_CHUNK
    L, G, K, N = SSD_CHUNK, SSD_GROUPS, SSD_HEADS_PER_GROUP, SSD_STATE
    xdt = (xh * dt[..., None]).reshape(bsz, nc, L, G, K, p)
    da = (dt * a).reshape(bsz, nc, L, G, K)
    bc = bm.reshape(bsz, nc, L, G, N)
    cc = cm.reshape(bsz, nc, L, G, N)
    xs = (jnp.moveaxis(xdt, 1, 0), jnp.moveaxis(da, 1, 0), jnp.moveaxis(bc, 1, 0), jnp.moveaxis(cc, 1, 0))
    causal = jnp.tril(jnp.ones((L, L), dtype=bool))[None, :, :, None, None]

    def step(state, inp):
        xdt_c, da_c, b_c, c_c = inp
        acum = jnp.cumsum(da_c, axis=1)
        seg = acum[:, :, None] - acum[:, None, :]
        decay = jnp.exp(jnp.where(causal, seg, -jnp.inf))
        cb = jnp.einsum('blgn,bsgn->blsg', c_c, b_c)
        y_diag = jnp.einsum('blsg,blsgk,bsgkp->blgkp', cb, decay, xdt_c)
        y_off = jnp.einsum('blgn,bgkpn,blgk->blgkp', c_c, state, jnp.exp(acum))
        decay_to_end = jnp.exp(acum[:, -1:] - acum)
        new_state = state * jnp.exp(acum[:, -1])[..., None, None] + \
            jnp.einsum('bsgn,bsgk,bsgkp->bgkpn', b_c, decay_to_end, xdt_c)
        return new_state, y_diag + y_off

    state0 = jnp.zeros((bsz, G, K, p, N), jnp.float32)
    _, ys = lax.scan(step, state0, xs)
    return jnp.moveaxis(ys, 0, 1).reshape(bsz, s, SSD_HEADS, p)


def multiscale_pool(p):
    bsz, s, _ = p.shape
    pg = p.astype(jnp.float32).reshape(bsz, s, POOL_GROUPS, POOL_GROUP_DIM)
    cs = jnp.pad(jnp.cumsum(pg, axis=1), ((0, 0), (1, 0), (0, 0), (0, 0)))
    t = jnp.arange(1, s + 1)
    outs = []
    for g, w in enumerate(POOL_WINDOWS):
        lo = jnp.maximum(t - w, 0)
        win_sum = cs[:, 1:, g] - cs[:, lo, g]
        cnt = jnp.minimum(t, w).astype(jnp.float32)
        outs.append(win_sum / cnt[None, :, None] - pg[:, :, g])
    return jnp.stack(outs, axis=2).astype(p.dtype)


def hybrid_mixer(u, w_in, conv_w, conv_b, dt_bias, a_log, d_skip, g_ssd_norm, w_ssd_out,
                 w_pool_grp, pool_scale, w_pool_out, w_out):
    bsz, s, _ = u.shape
    o1 = SSD_INNER
    o2 = o1 + SSD_CONV_DIM
    o3 = o2 + SSD_HEADS
    o4 = o3 + POOL_WIDTH
    proj = u @ w_in
    z, xbc, dt_raw, p_in, gates = jnp.split(proj, [o1, o2, o3, o4], axis=-1)

    xbc = jax.nn.silu(causal_dwconv(xbc, conv_w, conv_b))
    xs, bm, cm = jnp.split(xbc, [SSD_INNER, SSD_INNER + SSD_GROUPS * SSD_STATE], axis=-1)
    xh = xs.reshape(bsz, s, SSD_HEADS, SSD_HEAD_DIM)
    dt = jax.nn.softplus((dt_raw + dt_bias).astype(jnp.float32))
    a = -jnp.exp(a_log.astype(jnp.float32))
    y = ssd_chunked(xh.astype(jnp.float32), dt, a,
                    bm.reshape(bsz, s, SSD_GROUPS, SSD_STATE).astype(jnp.float32),
                    cm.reshape(bsz, s, SSD_GROUPS, SSD_STATE).astype(jnp.float32))
    y = y + d_skip.astype(jnp.float32)[:, None] * xh.astype(jnp.float32)
    y = y.reshape(bsz, s, SSD_INNER).astype(u.dtype)
    y = group_rmsnorm(y * jax.nn.silu(z), g_ssd_norm, SSD_GROUPS)
    y_a = y @ w_ssd_out

    pb = multiscale_pool(p_in)
    pb = jnp.einsum('bsgc,gcd->bsgd', pb, w_pool_grp).reshape(bsz, s, POOL_WIDTH) * pool_scale
    y_b = pb @ w_pool_out

    g_a, g_b = jnp.split(jax.nn.sigmoid(gates), 2, axis=-1)
    return (g_a * y_a + g_b * y_b) @ w_out


def conv_glu(v, w_ffn_in, conv_w, conv_b, w_ffn_out):
    val, gate = jnp.split(v @ w_ffn_in, 2, axis=-1)
    gate = causal_dwconv(gate, conv_w, conv_b)
    return (jax.nn.silu(gate) * val) @ w_ffn_out


def setup_inputs(seed: int = 0) -> dict:
    key = jax.random.key(seed)
    ks = jax.random.split(key, 24)
    f32 = jnp.float32
    nrm = lambda k, shape, scale: jax.random.normal(k, shape, f32) * scale
    dt0 = jnp.exp(jax.random.uniform(ks[8], (DEPTH, SSD_HEADS), f32, math.log(1e-3), math.log(1e-1)))
    return {
        "x": nrm(ks[0], (BATCH, SEQ, D_MODEL), 1.0),
        "c": nrm(ks[1], (BATCH, D_MODEL), 1.0),
        "w_ada": nrm(ks[2], (DEPTH, D_MODEL, N_MOD * D_MODEL), 0.5 * D_MODEL ** -0.5),
        "b_ada": nrm(ks[3], (DEPTH, N_MOD * D_MODEL), 0.02),
        "g_norm1": 1.0 + nrm(ks[4], (DEPTH, D_MODEL), 0.02),
        "w_in": nrm(ks[5], (DEPTH, D_MODEL, IN_WIDTH), D_MODEL ** -0.5),
        "ssd_conv_w": nrm(ks[6], (DEPTH, SSD_CONV, SSD_CONV_DIM), SSD_CONV ** -0.5),
        "ssd_conv_b": nrm(ks[7], (DEPTH, SSD_CONV_DIM), 0.02),
        "ssd_dt_bias": dt0 + jnp.log(-jnp.expm1(-dt0)),
        "ssd_a_log": jnp.log(jax.random.uniform(ks[9], (DEPTH, SSD_HEADS), f32, 1.0, 16.0)),
        "ssd_d": 1.0 + nrm(ks[10], (DEPTH, SSD_HEADS), 0.02),
        "g_ssd_norm": 1.0 + nrm(ks[11], (DEPTH, SSD_INNER), 0.02),
        "w_ssd_out": nrm(ks[12], (DEPTH, SSD_INNER, D_MODEL), SSD_INNER ** -0.5),
        "w_pool_grp": nrm(ks[13], (DEPTH, POOL_GROUPS, POOL_GROUP_DIM, POOL_GROUP_DIM), POOL_GROUP_DIM ** -0.5),
        "pool_scale": 1.0 + nrm(ks[14], (DEPTH, POOL_WIDTH), 0.02),
        "w_pool_out": nrm(ks[15], (DEPTH, POOL_WIDTH, D_MODEL), POOL_WIDTH ** -0.5),
        "w_out": nrm(ks[16], (DEPTH, D_MODEL, D_MODEL), D_MODEL ** -0.5),
        "g_norm2": 1.0 + nrm(ks[17], (DEPTH, D_MODEL), 0.02),
        "w_ffn_in": nrm(ks[18], (DEPTH, D_MODEL, 2 * FFN_HIDDEN), D_MODEL ** -0.5),
        "ffn_conv_w": nrm(ks[19], (DEPTH, FFN_CONV, FFN_HIDDEN), FFN_CONV ** -0.5),
        "ffn_conv_b": nrm(ks[20], (DEPTH, FFN_HIDDEN), 0.02),
        "w_ffn_out": nrm(ks[21], (DEPTH, FFN_HIDDEN, D_MODEL), FFN_HIDDEN ** -0.5),
        "g_final": 1.0 + nrm(ks[22], (D_MODEL,), 0.02),
    }


def reference(x, c, w_ada, b_ada, g_norm1, w_in, ssd_conv_w, ssd_conv_b, ssd_dt_bias, ssd_a_log,
              ssd_d, g_ssd_norm, w_ssd_out, w_pool_grp, pool_scale, w_pool_out, w_out, g_norm2,
              w_ffn_in, ffn_conv_w, ffn_conv_b, w_ffn_out, g_final):
    h = x
    cond = jax.nn.silu(c)
    for layer in range(DEPTH):
        mod = cond @ w_ada[layer] + b_ada[layer]
        shift1, scale1, gate1, shift2, scale2, gate2 = jnp.split(mod[:, None, :], N_MOD, axis=-1)
        u = rmsnorm(h, g_norm1[layer]) * (1.0 + scale1) + shift1
        h = h + gate1 * hybrid_mixer(u, w_in[layer], ssd_conv_w[layer], ssd_conv_b[layer],
                                     ssd_dt_bias[layer], ssd_a_log[layer], ssd_d[layer],
                                     g_ssd_norm[layer], w_ssd_out[layer], w_pool_grp[layer],
                                     pool_scale[layer], w_pool_out[layer], w_out[layer])
        v = rmsnorm(h, g_norm2[layer]) * (1.0 + scale2) + shift2
        h = h + gate2 * conv_glu(v, w_ffn_in[layer], ffn_conv_w[layer], ffn_conv_b[layer],
                                 w_ffn_out[layer])
    return rmsnorm(h, g_final)
```

```python
import numpy as np
from contextlib import ExitStack
import concourse.bass as bass
import concourse.mybir as mybir
from concourse.bass_utils import run_bass_kernel_spmd

F32 = mybir.dt.float32
BF16 = mybir.dt.bfloat16
AF = mybir.ActivationFunctionType
ALU = mybir.AluOpType

NCORES = 8
D = 1024
KC = 8
T = 2048
NPRE = 111
NTS = 18 + NPRE
HALO = 32
NH = 32
HD = 64
O1 = 2048
O2 = O1 + 3072
O3 = O2 + 32
O4 = O3 + 1024
FF = 2816
FFC = 22
EPS = 1e-6
SEM_CAP = 12000
_NC_CACHE = {}

PP_C = 0
PP_G1 = 8
PP_BSH1 = 16
PP_BSC1 = 24
PP_G2 = 32
PP_BSH2 = 40
PP_BSC2 = 48
PP_CW = 56
PP_CB = PP_CW + 96
PP_FW = PP_CB + 24
PP_FB = PP_FW + 66
PP_GSSD = PP_FB + 22
PP_PSC = PP_GSSD + 16
PP_HM = PP_PSC + 8
PP_SEL = PP_HM + 1
PP_MK = PP_SEL + 8
PP_MH = PP_MK + 28
NPP = PP_MH + 28
RW_GFIN = 0
RW_DTB = 1024
RW_ALOG = 1056
RW_DSK = 1088
RW_RC = 1120
NRW = RW_RC + 128


class _Stop(Exception):
    pass


KSTOP = None


class Arena:
    def __init__(self, t, ncols):
        self.t = t
        self.ncols = ncols
        self.recs = []
        self.pos = 0
        self.bank = None


class Buf:
    def __init__(self, arena, lo, n):
        self.arena = arena
        self.lo = lo
        self.n = n
        self.hi = lo + n

    def ap(self, a=0, b=None, p=128):
        b = self.n if b is None else b
        return self.arena.t[0:p, self.lo + a:self.lo + b]

    def sub(self, a, b):
        return Buf(self.arena, self.lo + a, b - a)


class BufV(Buf):
    def ap(self, a=0, b=None, p=128):
        b = 2 * self.n if b is None else b
        return self.arena.t[0:p, self.lo:self.hi].bitcast(BF16)[:, a:b]


class KB:
    ENGS = ("pe", "act", "dve", "pool", "sp")

    def __init__(self, nc, es, dry):
        self.nc = nc
        self.dry = dry
        self.q = {e: [] for e in self.ENGS}
        self.cnt = {e: 0 for e in self.ENGS}
        self.semi = {e: 0 for e in self.ENGS}
        self.sems = {}
        self.known = {e: {} for e in self.ENGS}
        self.ndsem = 10
        self.dsems = {}
        self.dcnt = {}
        self.dnext = {"sp": 0, "pool": 0, "act": 0}
        self.out_toks = []
        self.log = {e: [] for e in self.ENGS}
        if not dry:
            for e in self.ENGS:
                self.sems[e] = [es.enter_context(nc.semaphore("s_%s_%d" % (e, i))) for i in range(4)]
            for qn in ("sp", "pool"):
                self.dsems[qn] = [es.enter_context(nc.semaphore("d_%s_%d" % (qn, i))) for i in range(self.ndsem)]
                self.dcnt[qn] = [0] * self.ndsem

    def _wait(self, eng, tok):
        key, val, src = tok
        if src == eng and eng == "pe":
            return
        if self.known[eng].get(key, 0) >= val:
            return
        self.known[eng][key] = val
        sem = self._semobj(key)
        self.log[eng].append(('w', key, val))
        self.q[eng].append(lambda e, s=sem, v=val: e.wait_ge(s, v))

    def _semobj(self, key):
        kind, name, i = key
        if kind == "e":
            return self.sems[name][i]
        return self.dsems[name][i]

    @staticmethod
    def _rng(b):
        bk = b.arena.bank
        if bk is None:
            return b.lo, b.hi
        return (b.lo // bk) * bk, ((b.hi + bk - 1) // bk) * bk

    def _deps(self, eng, R, W):
        toks = []
        for (lst, isw) in ((R, False), (W, True)):
            for b in lst:
                lo, hi = self._rng(b)
                ps = b.arena.bank is not None
                for r in b.arena.recs:
                    if r[0] < hi and lo < r[1]:
                        if isw or r[2] or (ps and r[3][2] != eng):
                            toks.append(r[3])
        for t in toks:
            self._wait(eng, t)

    def _record(self, eng, R, W, tok):
        for (lst, isw) in ((R, False), (W, True)):
            for b in lst:
                a = b.arena
                lo, hi = self._rng(b)
                if isw or a.bank is not None:
                    a.recs = [r for r in a.recs if not (lo <= r[0] and r[1] <= hi)]
                else:
                    a.recs = [r for r in a.recs if not ((not r[2]) and r[3][2] == eng and r[3][0][0] == "e"
                                                        and tok[0][0] == "e" and lo <= r[0] and r[1] <= hi)]
                a.recs.append((lo, hi, isw, tok))

    def I(self, eng, fn, R=(), W=(), inc=True):
        if self.dry:
            return
        self._deps(eng, R, W)
        if inc:
            if self.cnt[eng] >= SEM_CAP:
                self.semi[eng] += 1
                self.cnt[eng] = 0
            self.cnt[eng] += 1
            val = self.cnt[eng]
            si = self.semi[eng]
            sem = self.sems[eng][si]
            self.log[eng].append(('i', ('e', eng, si), 1))
            self.q[eng].append(lambda e, fn=fn, sem=sem: fn(e).then_inc(sem, 1))
        else:
            val = self.cnt[eng] + 1
            si = self.semi[eng]
            assert val <= SEM_CAP
            self.log[eng].append(('n', None, 0))
            self.q[eng].append(lambda e, fn=fn: fn(e))
        tok = (("e", eng, si), val, eng)
        self._record(eng, R, W, tok)

    def Dm(self, qn, out_fn, in_fn, R=(), W=(), is_out=False, **kw):
        if self.dry:
            return
        self._deps(qn, R, W)
        j = self.dnext[qn]
        self.dnext[qn] = (j + 1) % self.ndsem
        key = ("d", qn, j)
        if self.dcnt[qn][j] > 0:
            self._wait(qn, (key, 16 * self.dcnt[qn][j], "dma"))
        self.dcnt[qn][j] += 1
        val = 16 * self.dcnt[qn][j]
        sem = self.dsems[qn][j]
        self.log[qn].append(('i', key, 16))
        self.q[qn].append(lambda e, o=out_fn, i=in_fn, sem=sem, kw=kw: e.dma_start(out=o(), in_=i(), **kw).then_inc(sem, 16))
        tok = (key, val, "dma")
        self._record("dma", R, W, tok)
        if is_out:
            self.out_toks.append(tok)

    def finish(self):
        if self.dry:
            return
        for t in self.out_toks:
            self._wait("sp", t)


def build_program():
    nc = bass.Bass("TRN2", target_bir_lowering=False)
    dr = {}

    def din(name, shape):
        dr[name] = nc.dram_tensor(name, shape, F32, kind="ExternalInput").ap()
        return dr[name]

    x_d = din("x", [NTS * 128, D])
    wada_d = din("w_ada", [D, 6 * D])
    win_d = din("w_in", [D, 8224])
    wsso_d = din("w_ssd_out", [2048, D])
    wpg_d = din("w_pool_grp", [1024, 256])
    wpo_d = din("w_pool_out", [D, D])
    wo_d = din("w_out", [D, D])
    wfi_d = din("w_ffn_in", [D, 2 * FF])
    wfo_d = din("w_ffn_out", [FF, D])
    pp_d = din("pp", [128, NPP])
    rw_d = din("rows", [128, NRW])
    bg_d = din("bgate", [128, 2048])
    cst_d = din("cst", [128, 384])
    y_d = nc.dram_tensor("y", [T, D], F32, kind="ExternalOutput").ap()
    cc_in = nc.dram_tensor("cc_in", [128, 2080], F32, kind="Internal").ap()
    cc_out = nc.dram_tensor("cc_out", [NCORES * 128, 2080], F32, kind="Internal").ap()

    A16N = 3472 + 12544 + 29952
    A32N = 12160 + 8256
    with ExitStack() as es:
        w16_t = es.enter_context(nc.sbuf_tensor("w16", [128, 3 * 4096], BF16))
        a16_t = es.enter_context(nc.sbuf_tensor("a16", [128, A16N], BF16))
        a32_t = es.enter_context(nc.sbuf_tensor("a32", [128, A32N], F32))
        ps_t = es.enter_context(nc.psum_tensor("ps", [128, 3584], F32))
        pt_t = es.enter_context(nc.psum_tensor("pt", [128, 1024], BF16))
        kb_real = KB(nc, es, False)
        plan = []
        for dry in (True, False):
            kb = KB(nc, es, True) if dry else kb_real
            arenas = dict(w16=Arena(w16_t, 3 * 4096), a16=Arena(a16_t, A16N), a32=Arena(a32_t, A32N),
                          ps=Arena(ps_t, 3584), pt=Arena(pt_t, 1024), cc=Arena(None, 4))
            try:
                arenas['ps'].bank = 512
                arenas['pt'].bank = 1024
                emit(nc, kb, arenas, dr, y_d, cc_in, cc_out, plan, dry)
            except _Stop:
                pass
        kb = kb_real
        kb.finish()
        _NC_CACHE['kb'] = kb
        block = es.enter_context(nc.Block())

        @block.sync
        def _(e):
            for f in kb.q["sp"]:
                f(e)

        @block.gpsimd
        def _(e):
            for f in kb.q["pool"]:
                f(e)

        @block.tensor
        def _(e):
            for f in kb.q["pe"]:
                f(e)

        @block.vector
        def _(e):
            for f in kb.q["dve"]:
                f(e)

        @block.scalar
        def _(e):
            for f in kb.q["act"]:
                f(e)
    return nc


def emit(nc, kb, AR, dr, y_d, cc_in, cc_out, plan, dry):
    I = kb.I
    x_d = dr["x"]

    def mark(name):
        if KSTOP is not None and name == KSTOP:
            raise _Stop()

    def alloc(an, n):
        a = AR[an]
        b = Buf(a, a.pos, n)
        a.pos += n
        assert a.pos <= a.ncols, (an, a.pos, a.ncols)
        return b

    wslots = [alloc("w16", 4096) for _ in range(3)]
    S_bf = alloc("a16", 2048)
    ident_bf = alloc("a16", 128)
    vhalo = alloc("a16", 256)
    cond_bf = alloc("a16", 16)
    cond_bc = alloc("a16", 1024)
    AR["a16"].pos = 3472
    uT = alloc("a16", 8 * 544)
    ynT = alloc("a16", 16 * 512)
    X16 = AR["a16"].pos
    xBCc = alloc("a16", 24 * 512)
    siluz = alloc("a16", 4 * 2048)
    x_tm = alloc("a16", 2048)
    xw = alloc("a16", 2048)
    xD = alloc("a16", 2048)
    B_tm = alloc("a16", 512)
    Mbs = [alloc("a16", 1024) for _ in range(2)]
    ynb = alloc("a16", 512)
    AR["a16"].pos = X16
    pbT = alloc("a16", 8 * 512)
    pgT = alloc("a16", 8 * 512)
    mergedT = alloc("a16", 8 * 512)
    hn = alloc("a16", 1024)
    hn2 = alloc("a16", 1024)
    hnb = [hn, hn2]
    vT = alloc("a16", 8 * 544)
    hiddenT = alloc("a16", 22 * 512)

    S32 = alloc("a32", 2048)
    g1bc = alloc("a32", 1024)
    g2bc = alloc("a32", 1024)
    rows = alloc("a32", NRW)
    cst = alloc("a32", 384)
    pp = alloc("a32", NPP)
    modp = alloc("a32", 32)
    a_bc = alloc("a32", 32)
    totsum = alloc("a32", 32)
    Dj = alloc("a32", 32)
    smA = [alloc("a32", 128) for _ in range(11)]
    smB = [alloc("a32", 128) for _ in range(11)]
    smsets = [smA, smB]
    sm = smA
    ssq = alloc("a32", 16)
    xt = [alloc("a32", 1024) for _ in range(3)]
    assert AR["a32"].pos <= 12160, AR["a32"].pos
    AR["a32"].pos = 12160
    Y32 = 12160
    CBm = alloc("a32", 512)
    Rbs = [alloc("a32", 1024) for _ in range(2)]
    Ebs = [alloc("a32", 1024) for _ in range(2)]
    sqj = alloc("a32", 512)
    tb = [alloc("a32", 512) for _ in range(2)]
    cacc = [alloc("a32", 544) for _ in range(2)]
    AR["a32"].pos = Y32
    Gx = [alloc("a32", 2080) for _ in range(2)]
    AR["a32"].pos = Y32
    bgt = alloc("a32", 2048)
    AR["a32"].pos = Y32
    pT = alloc("a32", 8 * 544)
    spp = [alloc("a32", 544) for _ in range(2)]
    gab = alloc("a32", 512)
    gbb = alloc("a32", 512)
    m1b = alloc("a32", 512)
    AR["a32"].pos = Y32
    hbuf = alloc("a32", 4096)
    tg = [alloc("a32", 512) for _ in range(2)]
    gacc = [alloc("a32", 544) for _ in range(2)]
    outb = [alloc("a32", 1024) for _ in range(2)]

    PS = AR["ps"]
    PT = Buf(AR["pt"], 0, 1024)
    D0 = Buf(PS, 0, 512)
    D1 = Buf(PS, 512, 512)
    Qb = Buf(PS, 1024, 512)
    ABb = Buf(PS, 1536, 1024)
    YD = Buf(PS, 2560, 512)
    YO = Buf(PS, 3072, 512)
    WIDE = [Buf(PS, 0, 1024), Buf(PS, 1024, 1024), Buf(PS, 2048, 1024)]
    PT2 = BufV(PS, 2560, 512)
    PTS = [PT, PT2]
    st = dict(d=0, wide=0, wi=0, xt=0, pt=0, pt2=False, sm=None)

    def nextPT():
        if not st["pt2"]:
            return PT
        st["pt"] ^= 1
        return PTS[st["pt"]]

    def nextD():
        st["d"] ^= 1
        return (D0, D1)[st["d"]]

    def nextWide():
        st["wide"] = (st["wide"] + 1) % 3
        return WIDE[st["wide"]]

    ident32 = cst.sub(0, 128)
    tri32 = cst.sub(128, 256)
    ones32 = cst.sub(256, 384)

    def ppc(col, p=128):
        return pp.ap(col, col + 1, p)

    def wpiece(parts):
        idx = st["wi"]
        st["wi"] += 1
        if dry:
            plan.append(parts)
            return wslots[idx % 3]

        def issue(k):
            if k >= len(plan) or k < st.get("issued", 0):
                return
            st["issued"] = k + 1
            slot = wslots[k % 3]
            off = 0
            for src in plan[k]:
                rows_, nco = src.shape
                kc = rows_ // 128
                dst = slot.sub(off, off + kc * nco)
                kb.Dm("pool",
                      (lambda dst=dst, kc=kc: dst.ap().rearrange("p (k c) -> p k c", k=kc)),
                      (lambda src=src: src.rearrange("(k p) c -> p k c", p=128)),
                      W=[dst])
                off += kc * nco
        for k in range(st.get("issued", 0), idx + 2):
            issue(k)
        return wslots[idx % 3]

    def act(out, in_, func, R, W, **kw):
        I("act", lambda e: e.activation(out=out, in_=in_, func=func, **kw), R, W)

    def tt(out, in0, in1, op, R, W, eng="dve"):
        I(eng, lambda e: e.tensor_tensor(out=out, in0=in0, in1=in1, op=op), R, W)

    def ts(out, in0, s1, op0, R, W, s2=None, op1=None, eng="dve"):
        if op1 is None:
            I(eng, lambda e: e.tensor_scalar(out=out, in0=in0, scalar1=s1, scalar2=None, op0=op0), R, W)
        else:
            I(eng, lambda e: e.tensor_scalar(out=out, in0=in0, scalar1=s1, scalar2=s2, op0=op0, op1=op1), R, W)

    def stt(out, in0, sc, in1, op0, op1, R, W):
        I("dve", lambda e: e.scalar_tensor_tensor(out=out, in0=in0, scalar=sc, in1=in1, op0=op0, op1=op1), R, W)

    def mm(outb_, out, lhsT, rhs, start, stop, R, inc):
        I("pe", lambda e: e.matmul(out, lhsT=lhsT, rhs=rhs, start=start, stop=stop), R, [outb_], inc=inc)

    def tr(outb_, out, in_, idn, R, inc=True):
        I("pe", lambda e: e.transpose(out, in_, idn), R, [outb_], inc=inc)

    def dma_in(dstb, dst_fn, src_fn):
        kb.Dm("sp", dst_fn, src_fn, W=[dstb])

    dma_in(cst, lambda: cst.ap(), lambda: dr["cst"][:, :])
    dma_in(pp, lambda: pp.ap(), lambda: dr["pp"][:, :])
    dma_in(rows, lambda: rows.ap(), lambda: dr["rows"][:, :])
    dma_in(bgt, lambda: bgt.ap(), lambda: dr["bgate"][:, :])
    I("dve", lambda e: e.tensor_copy(out=ident_bf.ap(), in_=ident32.ap()), [ident32], [ident_bf])
    act(sm[0].ap(0, 8), pp.ap(PP_C, PP_C + 8), AF.Silu, [pp], [sm[0]])
    I("dve", lambda e: e.tensor_copy(out=cond_bf.ap(0, 8), in_=sm[0].ap(0, 8)), [sm[0]], [cond_bf])
    for k in range(8):
        I("dve", lambda e, k=k: e.tensor_copy(out=cond_bc.ap(k * 128, (k + 1) * 128),
                                              in_=sm[0].ap(k, k + 1).broadcast_to([128, 128])),
          [sm[0]], [cond_bc.sub(k * 128, (k + 1) * 128)])
    act(a_bc.ap(), rows.ap(RW_ALOG, RW_ALOG + 32), AF.Exp, [rows], [a_bc])
    ts(a_bc.ap(), a_bc.ap(), -1.0, ALU.mult, [a_bc], [a_bc])
    wada = dr["w_ada"]
    for (c0, dstcol, bcol, gcol, is_scale) in ((0, 8, PP_BSH1, None, False), (1024, 0, PP_BSC1, PP_G1, True),
                                               (3072, 24, PP_BSH2, None, False), (4096, 16, PP_BSC2, PP_G2, True)):
        psd = nextD()
        for half in range(2):
            slot = wpiece([wada[:, c0 + half * 512:c0 + (half + 1) * 512]])
            for jj in range(4):
                j = half * 4 + jj
                for k in range(8):
                    mm(psd, psd.ap(j, j + 1), slot.ap(k * 512 + jj * 128, k * 512 + (jj + 1) * 128),
                       cond_bf.ap(k, k + 1), k == 0, k == 7, [slot, cond_bf], inc=(k == 7))
        d = modp.sub(dstcol, dstcol + 8)
        tt(d.ap(), psd.ap(0, 8), pp.ap(bcol, bcol + 8), ALU.add, [psd, pp], [d])
        if is_scale:
            ts(d.ap(), d.ap(), 1.0, ALU.add, [d], [d])
            tt(d.ap(), d.ap(), pp.ap(gcol, gcol + 8), ALU.mult, [d, pp], [d])
    for (c0, gb_, boff) in ((2048, g1bc, 0), (5120, g2bc, 1024)):
        for half in range(2):
            slot = wpiece([wada[:, c0 + half * 512:c0 + (half + 1) * 512]])
            psd = nextD()
            for k in range(8):
                mm(psd, psd.ap(), cond_bc.ap(k * 128, (k + 1) * 128), slot.ap(k * 512, (k + 1) * 512),
                   k == 0, k == 7, [slot, cond_bc], inc=(k == 7))
            d = gb_.sub(half * 512, (half + 1) * 512)
            tt(d.ap(), psd.ap(), bgt.ap(boff + half * 512, boff + (half + 1) * 512), ALU.add, [psd, bgt], [d])

    A1, SH1, A2, SH2 = 0, 8, 16, 24
    st["sm"] = smA
    mark("prologue")

    def pump(ch, n=1):
        for _ in range(n):
            if ch:
                ch.pop(0)()

    def norm_A(src_ap_fn, srcb, np_, slot_):
        hb_ = hnb[slot_]
        sq_ = ssq.sub(8 + slot_ * 4, 12 + slot_ * 4)
        act(hb_.ap(0, 1024, np_), src_ap_fn(), AF.Square, [srcb], [hb_, sq_], accum_out=sq_.ap(0, 1, np_))
        act(sq_.ap(1, 2, np_), sq_.ap(0, 1, np_), AF.Sqrt, [sq_], [sq_], scale=1.0 / D, bias=EPS)
        I("dve", lambda e: e.reciprocal(out=sq_.ap(2, 3, np_), in_=sq_.ap(1, 2, np_)), [sq_], [sq_])
        ts(hb_.ap(0, 1024, np_), src_ap_fn(), sq_.ap(2, 3, np_), ALU.mult, [srcb, sq_], [hb_])

    def norm_B(np_, slot_, dstT, dcol, acol, shcol, width):
        hb_ = hnb[slot_]
        for k in range(8):
            P_ = PT if k < 4 else PT2
            kk = k % 4
            tr(P_, P_.ap(kk * 128, kk * 128 + np_), hb_.ap(k * 128, (k + 1) * 128, np_), ident_bf.ap(0, np_, np_),
               [hb_, ident_bf], inc=(kk == 3))
        for k in range(4):
            d = dstT.sub(k * width + dcol, k * width + dcol + np_)
            act(d.ap(), PT.ap(k * 128, k * 128 + np_), AF.Identity, [PT, modp], [d],
                scale=modp.ap(acol + k, acol + k + 1), bias=modp.ap(shcol + k, shcol + k + 1))
            k2 = k + 4
            d2 = dstT.sub(k2 * width + dcol, k2 * width + dcol + np_)
            ts(d2.ap(), PT2.ap(k * 128, k * 128 + np_), modp.ap(acol + k2, acol + k2 + 1), ALU.mult, [PT2, modp], [d2],
               s2=modp.ap(shcol + k2, shcol + k2 + 1), op1=ALU.add)

    def norm_tile(src_ap_fn, srcb, np_, dstT, dcol, acol, shcol, width):
        norm_A(src_ap_fn, srcb, np_, 0)
        norm_B(np_, 0, dstT, dcol, acol, shcol, width)

    def load_x(row0, np_):
        b = xt[st["xt"]]
        st["xt"] = (st["xt"] + 1) % 3
        dma_in(b, lambda: b.ap(0, 1024, np_), lambda: x_d[row0:row0 + np_, :])
        return b

    def stage_u_thunks(tiles, width):
        t0 = tiles[0]
        items = [(t0 * 128 - HALO, HALO, 0)] + [(tl * 128, 128, HALO + i * 128) for i, tl in enumerate(tiles)]
        bufs_ = []
        th = []

        def first():
            bufs_.append(load_x(items[0][0], items[0][1]))
            if len(items) > 1:
                bufs_.append(load_x(items[1][0], items[1][1]))
            norm_A((lambda b=bufs_[0], n=items[0][1]: b.ap(0, 1024, n)), bufs_[0], items[0][1], 0)
        th.append(first)
        for j, (row0, np_, dcol) in enumerate(items):
            def step(j=j, np_=np_, dcol=dcol):
                if j + 2 < len(items):
                    bufs_.append(load_x(items[j + 2][0], items[j + 2][1]))
                if j + 1 < len(items):
                    nb_ = bufs_[j + 1]
                    norm_A((lambda b=nb_, n=items[j + 1][1]: b.ap(0, 1024, n)), nb_, items[j + 1][1], (j + 1) % 2)
                norm_B(np_, j % 2, uT, dcol, A1, SH1, width)
            th.append(step)
        return th

    def stage_u(tiles, width):
        pump(stage_u_thunks(tiles, width), 99)

    def formA(wide, slot, soff, kc, ncol_w, j, rhsT, width, ncols, rbufs):
        segs = [(0, min(512, ncols))]
        if ncols > 512:
            segs.append((512, ncols))
        for (a, b) in segs:
            for k in range(kc):
                mm(wide.sub(a, b), wide.ap(a, b),
                   slot.ap(soff + k * ncol_w + j * 128, soff + k * ncol_w + (j + 1) * 128),
                   rhsT.ap(k * width + a, k * width + b), k == 0, k == kc - 1, [slot] + rbufs, inc=(k == kc - 1))

    def conv_p1(wide, ch, TB, premask, hcol=None):
        W_ = HALO + TB
        if hcol is not None:
            act(wide.ap(0, HALO), wide.ap(0, HALO), AF.Identity, [wide, pp], [wide.sub(0, HALO)], scale=ppc(hcol))
        if premask > 0:
            ts(wide.ap(0, premask), wide.ap(0, premask), ppc(PP_HM), ALU.mult, [wide, pp], [wide.sub(0, premask)])
        ca = cacc[ch % 2]
        act(ca.ap(0, TB), wide.ap(HALO, W_), AF.Identity, [wide, pp], [ca],
            scale=ppc(PP_CW + ch * 4 + 3), bias=ppc(PP_CB + ch))
        for kk in (1, 2, 3):
            stt(ca.ap(0, TB), wide.ap(HALO - kk, W_ - kk), ppc(PP_CW + ch * 4 + 3 - kk), ca.ap(0, TB),
                ALU.mult, ALU.add, [wide, pp, ca], [ca])

    def conv_p2(ch, TB, postmask):
        ca = cacc[ch % 2]
        d = xBCc.sub(ch * 512, ch * 512 + TB)
        act(d.ap(), ca.ap(0, TB), AF.Silu, [ca], [d])
        if postmask > 0:
            dd = xBCc.sub(ch * 512, ch * 512 + postmask)
            ts(dd.ap(), dd.ap(), ppc(PP_HM), ALU.mult, [dd, pp], [dd])

    win = dr["w_in"]

    def inproj_xbc(nt, chunks_pieces, premask, postmask, hcol=None, chain=None):
        TB = nt * 128
        W_ = HALO + TB
        prev = None
        for (c0, ch0) in chunks_pieces:
            slot = wpiece([win[:, c0:c0 + 512]])
            for j in range(4):
                wide = nextWide()
                formA(wide, slot, 0, 8, 512, j, uT, W_, W_, [uT])
                conv_p1(wide, ch0 + j, TB, premask, hcol)
                if prev is not None:
                    conv_p2(prev, TB, postmask)
                prev = ch0 + j
        conv_p2(prev, TB, postmask)

    def dt_chain(nt, full, slot, mkcol=None, blocklevel=False, smset=None):
        smset = st["sm"] if smset is None else smset
        raw, m_, na, da, lndt, acum, bias_s, w2, expA, dec, tmp = [b_.sub(0, nt * 32) for b_ in smset[0:11]]
        W_ = HALO + nt * 128
        n32 = nt * 32
        ch = []
        hold = {}

        def v3(b_):
            return b_.ap().rearrange("p (t h) -> p t h", t=nt)

        def s1():
            hold["psd"] = YO.sub(0, 128)
            psd = hold["psd"]
            for i in range(nt):
                for k in range(8):
                    mm(psd, psd.ap(i * 32, (i + 1) * 32), uT.ap(k * W_ + HALO + i * 128, k * W_ + HALO + (i + 1) * 128),
                       slot.ap(k * 32, (k + 1) * 32), k == 0, k == 7, [uT, slot], inc=(k == 7))
        ch.append(s1)
        ch.append(lambda: tt(v3(raw), hold["psd"].ap(0, n32).rearrange("p (t h) -> p t h", t=nt),
                             rows.ap(RW_DTB, RW_DTB + 32).unsqueeze(1).broadcast_to([128, nt, 32]), ALU.add,
                             [hold["psd"], rows], [raw]))
        ch.append(lambda: ts(m_.ap(), raw.ap(), 0.0, ALU.max, [raw], [m_]))
        ch.append(lambda: ts(na.ap(), raw.ap(), -1.0, ALU.mult, [raw], [na]))
        ch.append(lambda: tt(na.ap(), na.ap(), raw.ap(), ALU.min, [na, raw], [na]))
        ch.append(lambda: act(na.ap(), na.ap(), AF.Exp, [na], [na]))
        ch.append(lambda: act(na.ap(), na.ap(), AF.Ln, [na], [na], bias=1.0))
        ch.append(lambda: tt(raw.ap(), m_.ap(), na.ap(), ALU.add, [m_, na], [raw]))
        ch.append(lambda: tt(v3(da), v3(raw), a_bc.ap().unsqueeze(1).broadcast_to([128, nt, 32]), ALU.mult,
                             [raw, a_bc], [da]))
        ch.append(lambda: act(lndt.ap(), raw.ap(), AF.Ln, [raw], [lndt]))

        def s2():
            hold["psa"] = YO.sub(128, 384)
            psa = hold["psa"]
            for i in range(nt):
                mm(psa, psa.ap(i * 32, (i + 1) * 32), tri32.ap(), da.ap(i * 32, (i + 1) * 32), True, True,
                   [tri32, da], inc=True)
                mm(psa, psa.ap(128 + i * 32, 128 + (i + 1) * 32), ones32.ap(), da.ap(i * 32, (i + 1) * 32), True, True,
                   [ones32, da], inc=True)
        ch.append(s2)
        ch.append(lambda: I("dve", lambda e: e.tensor_copy(out=acum.ap(), in_=hold["psa"].ap(0, n32)),
                            [hold["psa"]], [acum]))
        ch.append(lambda: tt(bias_s.ap(), acum.ap(), lndt.ap(), ALU.subtract, [acum, lndt], [bias_s]))
        ch.append(lambda: tt(tmp.ap(), hold["psa"].ap(128, 128 + n32), bias_s.ap(), ALU.subtract,
                             [hold["psa"], bias_s], [tmp]))
        if blocklevel:
            def sfx():
                psa = hold["psa"]
                run = smset[8].sub(0, 32)
                I("dve", lambda e: e.tensor_copy(out=run.ap(), in_=psa.ap(128 + (nt - 1) * 32, 128 + nt * 32)), [psa], [run])
                for i in range(nt - 2, -1, -1):
                    ti = tmp.sub(i * 32, (i + 1) * 32)
                    tt(ti.ap(), ti.ap(), run.ap(), ALU.add, [ti, run], [ti])
                    tt(run.ap(), run.ap(), psa.ap(128 + i * 32, 128 + (i + 1) * 32), ALU.add, [run, psa], [run])
                act(dec.ap(0, 32), run.ap(), AF.Exp, [run], [dec])
            ch.append(sfx)
            ch.append(lambda: act(w2.ap(), tmp.ap(), AF.Exp, [tmp], [w2]))
        else:
            ch.append(lambda: act(w2.ap(), tmp.ap(), AF.Exp, [tmp], [w2]))
            ch.append(lambda: act(dec.ap(), hold["psa"].ap(128, 128 + n32), AF.Exp, [hold["psa"]], [dec]))
        if full:
            ch.append(lambda: act(expA.ap(), acum.ap(), AF.Exp, [acum], [expA]))
        if mkcol is not None:
            ch.append(lambda: ts(w2.ap(), w2.ap(), ppc(mkcol), ALU.mult, [w2, pp], [w2]))
        return ch

    def xB_transposes(i, mkcol=None):
        raw, m_, na, da, lndt, acum, bias_s, w2, expA, dec, tmp = [b_.sub(i * 32, (i + 1) * 32) for b_ in st["sm"][0:11]]
        for r in range(2):
            PT = nextPT()
            for j in range(8):
                ch = r * 8 + j
                tr(PT, PT.ap(j * 128, (j + 1) * 128), xBCc.ap(ch * 512 + i * 128, ch * 512 + (i + 1) * 128),
                   ident_bf.ap(), [xBCc, ident_bf], inc=(j == 7))
            d = x_tm.sub(r * 1024, (r + 1) * 1024)
            I("act", lambda e, d=d, PT=PT: e.copy(out=d.ap(), in_=PT.ap()), [PT], [d])
            d2 = xw.sub(r * 1024, (r + 1) * 1024)
            tt(d2.ap().rearrange("p (h c) -> p h c", h=16), d.ap().rearrange("p (h c) -> p h c", h=16),
               w2.ap(r * 16, (r + 1) * 16).unsqueeze(2).broadcast_to([128, 16, 64]), ALU.mult, [d, w2], [d2], eng="pool")
            if mkcol is None:
                d3 = xD.sub(r * 1024, (r + 1) * 1024)
                tt(d3.ap().rearrange("p (h c) -> p h c", h=16), d.ap().rearrange("p (h c) -> p h c", h=16),
                   rows.ap(RW_DSK + r * 16, RW_DSK + (r + 1) * 16).unsqueeze(2).broadcast_to([128, 16, 64]),
                   ALU.mult, [d, rows], [d3], eng="pool")
        PT = nextPT()
        for j in range(4):
            ch = 16 + j
            tr(PT, PT.ap(j * 128, (j + 1) * 128), xBCc.ap(ch * 512 + i * 128, ch * 512 + (i + 1) * 128),
               ident_bf.ap(), [xBCc, ident_bf], inc=(j == 3))
        I("act", lambda e, PT=PT: e.copy(out=B_tm.ap(), in_=PT.ap(0, 512)), [PT], [B_tm])

    def state_update(i):
        dec = st["sm"][9].sub(i * 32, (i + 1) * 32)
        for g in range(4):
            psd = nextD()
            mm(psd, psd.ap(), B_tm.ap(g * 128, (g + 1) * 128), xw.ap(g * 512, (g + 1) * 512), True, True,
               [B_tm, xw], inc=True)
            sg = S32.sub(g * 512, (g + 1) * 512)
            tt(sg.ap().rearrange("p (h c) -> p h c", h=8), sg.ap().rearrange("p (h c) -> p h c", h=8),
               dec.ap(g * 8, (g + 1) * 8).unsqueeze(2).broadcast_to([128, 8, 64]), ALU.mult, [sg, dec], [sg], eng="pool")
            tt(sg.ap(), sg.ap(), psd.ap(), ALU.add, [sg, psd], [sg])
            sb = S_bf.sub(g * 512, (g + 1) * 512)
            I("act", lambda e, sb=sb, sg=sg: e.copy(out=sb.ap(), in_=sg.ap()), [sg], [sb])

    SACC = [D0, D1, Qb, ABb.sub(0, 512)]

    def state_accum(i, nt):
        for g in range(4):
            mm(SACC[g], SACC[g].ap(), B_tm.ap(g * 128, (g + 1) * 128), xw.ap(g * 512, (g + 1) * 512), i == 0, i == nt - 1,
               [B_tm, xw], inc=True)

    def state_finish():
        dec = st["sm"][9].sub(0, 32)
        for g in range(4):
            sg = S32.sub(g * 512, (g + 1) * 512)
            tt(sg.ap().rearrange("p (h c) -> p h c", h=8), sg.ap().rearrange("p (h c) -> p h c", h=8),
               dec.ap(g * 8, (g + 1) * 8).unsqueeze(2).broadcast_to([128, 8, 64]), ALU.mult, [sg, dec], [sg], eng="pool")
            tt(sg.ap(), sg.ap(), SACC[g].ap(), ALU.add, [sg, SACC[g]], [sg])

    def ssd_y(i):
        raw, m_, na, da, lndt, acum, bias_s, w2, expA, dec, tmp = [b_.sub(i * 32, (i + 1) * 32) for b_ in st["sm"][0:11]]
        c0, c1 = i * 128, (i + 1) * 128
        for g in range(4):
            mm(Qb, Qb.ap(g * 128, (g + 1) * 128), xBCc.ap((16 + g) * 512 + c0, (16 + g) * 512 + c1),
               xBCc.ap((20 + g) * 512 + c0, (20 + g) * 512 + c1), True, True, [xBCc], inc=(g == 3))
        tt(CBm.ap().rearrange("p (g c) -> p g c", g=4), Qb.ap().rearrange("p (g c) -> p g c", g=4),
           tri32.ap().unsqueeze(1).broadcast_to([128, 4, 128]), ALU.mult, [Qb, tri32], [CBm])

        def R_(g):
            Rb = Rbs[g % 2]
            tt(Rb.ap().rearrange("p (h c) -> p h c", h=8),
               acum.ap(g * 8, (g + 1) * 8).unsqueeze(2).broadcast_to([128, 8, 128]),
               ident32.ap().unsqueeze(1).broadcast_to([128, 8, 128]), ALU.mult, [acum, ident32], [Rb], eng="pool")

        def AB_(g):
            Rb = Rbs[g % 2]
            for hh in range(2):
                mm(ABb.sub(hh * 512, (hh + 1) * 512), ABb.ap(hh * 512, (hh + 1) * 512), ones32.ap(),
                   Rb.ap(hh * 512, (hh + 1) * 512), True, True, [ones32, Rb], inc=True)

        def SUB_(g):
            Eb = Ebs[g % 2]
            tt(Eb.ap().rearrange("p (h c) -> p h c", h=8), ABb.ap().rearrange("p (h c) -> p h c", h=8),
               bias_s.ap(g * 8, (g + 1) * 8).unsqueeze(2).broadcast_to([128, 8, 128]), ALU.subtract,
               [ABb, bias_s], [Eb])

        def EXP_(g):
            Eb = Ebs[g % 2]
            act(Eb.ap(), Eb.ap(), AF.Exp, [Eb], [Eb])

        def M_(g):
            Eb, Mb = Ebs[g % 2], Mbs[g % 2]
            stt(Mb.ap().rearrange("p (h c) -> p h c", h=8), Eb.ap().rearrange("p (h c) -> p h c", h=8), 1e30,
                CBm.ap(g * 128, (g + 1) * 128).unsqueeze(1).broadcast_to([128, 8, 128]), ALU.min, ALU.mult,
                [Eb, CBm], [Mb])

        def Y_(g):
            Mb = Mbs[g % 2]
            mm(YD, YD.ap(), ident_bf.ap(), xD.ap(g * 512, (g + 1) * 512), True, False, [ident_bf, xD], inc=False)
            for h in range(8):
                hh = g * 8 + h
                mm(YD, YD.ap(h * 64, (h + 1) * 64), Mb.ap(h * 128, (h + 1) * 128), x_tm.ap(hh * 64, (hh + 1) * 64),
                   False, h == 7, [Mb, x_tm], inc=(h == 7))
            mm(YO, YO.ap(), xBCc.ap((20 + g) * 512 + c0, (20 + g) * 512 + c1), S_bf.ap(g * 512, (g + 1) * 512),
               True, True, [xBCc, S_bf], inc=True)

        def T_(g):
            t_ = tb[g % 2]
            tt(t_.ap().rearrange("p (h c) -> p h c", h=8), YO.ap().rearrange("p (h c) -> p h c", h=8),
               expA.ap(g * 8, (g + 1) * 8).unsqueeze(2).broadcast_to([128, 8, 64]), ALU.mult, [YO, expA], [t_])
            tt(t_.ap(), YD.ap(), t_.ap(), ALU.add, [YD, t_], [t_])
            sz = siluz.sub(i * 2048 + g * 512, i * 2048 + (g + 1) * 512)
            tt(t_.ap(), t_.ap(), sz.ap(), ALU.mult, [t_, sz], [t_])

        def N_(g):
            t_ = tb[g % 2]
            sq_ = ssq.sub(4 + (g % 2) * 4, 8 + (g % 2) * 4) if False else ssq.sub(4, 8)
            act(sqj.ap(), t_.ap(), AF.Square, [t_], [sqj, sq_], accum_out=sq_.ap(0, 1))
            act(sq_.ap(1, 2), sq_.ap(0, 1), AF.Ln, [sq_], [sq_], scale=1.0 / 512, bias=EPS)
            act(sq_.ap(2, 3), sq_.ap(1, 2), AF.Exp, [sq_], [sq_], scale=-0.5)

        def YN_(g):
            t_ = tb[g % 2]
            sq_ = ssq.sub(4, 8)
            ts(ynb.ap(), t_.ap(), sq_.ap(2, 3), ALU.mult, [t_, sq_], [ynb])

        def TR_(g):
            for j in range(4):
                tr(PT, PT.ap(j * 128, (j + 1) * 128), ynb.ap(j * 128, (j + 1) * 128), ident_bf.ap(),
                   [ynb, ident_bf], inc=(j == 3))
            for j in range(4):
                ch = g * 4 + j
                d = ynT.sub(ch * 512 + c0, ch * 512 + c1)
                act(d.ap(), PT.ap(j * 128, (j + 1) * 128), AF.Identity, [PT, pp], [d], scale=ppc(PP_GSSD + ch))

        R_(0)
        AB_(0)
        SUB_(0)
        EXP_(0)
        M_(0)
        for g in range(4):
            nx = g + 1 < 4
            if nx:
                R_(g + 1)
            Y_(g)
            if nx:
                AB_(g + 1)
            T_(g)
            if nx:
                SUB_(g + 1)
            N_(g)
            if nx:
                EXP_(g + 1)
            YN_(g)
            if nx:
                M_(g + 1)
            TR_(g)

    I("dve", lambda e: e.memset(S32.ap(), 0.0), [], [S32])
    I("dve", lambda e: e.memset(S_bf.ap(), 0.0), [], [S_bf])
    I("dve", lambda e: e.memset(totsum.ap(), 0.0), [], [totsum])
    I("dve", lambda e: e.memset(vhalo.ap(), 0.0), [], [vhalo])
    XB_PIECES = [(O1 + 512 * q, 4 * q) for q in range(5)]
    ALL_PIECES = [(O1 + 512 * q, 4 * q) for q in range(6)]
    nblk_pre = (NPRE + 3) // 4

    def pre_tiles(blk):
        return [1 + blk * 4 + q for q in range(4) if 1 + blk * 4 + q <= NPRE]

    blocks2 = [[NPRE + 1]] + [[NPRE + 2 + 4 * b + q for q in range(4)] for b in range(4)]
    st["pt2"] = True
    stage_u(pre_tiles(0), HALO + 4 * 128)
    slot0 = wpiece([win[:, O2:O2 + 32]])
    pump(dt_chain(4, False, slot0, mkcol=PP_MK + 0, blocklevel=True, smset=smsets[0]), 99)
    for blk in range(nblk_pre):
        tiles = pre_tiles(blk)
        nt = len(tiles)
        st["W"] = HALO + nt * 128
        st["sm"] = smsets[blk % 2]
        inproj_xbc(nt, XB_PIECES, 0, 0, hcol=PP_MH + blk)
        last = blk + 1 >= nblk_pre
        nxt = blocks2[0] if last else pre_tiles(blk + 1)
        sut = stage_u_thunks(nxt, HALO + len(nxt) * 128)
        slotn = wpiece([win[:, O2:O2 + 32]])
        if last:
            chn = dt_chain(len(nxt), True, slotn, smset=smsets[(blk + 1) % 2])
        else:
            chn = dt_chain(len(nxt), False, slotn, mkcol=PP_MK + blk + 1, blocklevel=True, smset=smsets[(blk + 1) % 2])
        for i in range(nt):
            xB_transposes(i, mkcol=PP_MK + blk)
            state_accum(i, nt)
            pump(sut, 2)
            if not sut:
                pump(chn, 6)
        pump(sut, 99)
        pump(chn, 99)
        state_finish()
        mark("p1b%d" % blk)
    st["pt2"] = False
    I("act", lambda e: e.copy(out=S_bf.ap(), in_=S32.ap()), [S32], [S_bf])
    mark("exch")
    wsso, wpg, wpo, wo, wfi, wfo = (dr["w_ssd_out"], dr["w_pool_grp"], dr["w_pool_out"], dr["w_out"],
                                    dr["w_ffn_in"], dr["w_ffn_out"])
    for bi, tiles in enumerate(blocks2):
        nt = len(tiles)
        TB = nt * 128
        W_ = HALO + TB
        st["W"] = W_
        if bi > 0:
            stage_u(tiles, W_)
        if bi == 0:
            st["sm"] = smsets[nblk_pre % 2]
        else:
            st["sm"] = smsets[0]
            slot = wpiece([win[:, O2:O2 + 32]])
            chain = dt_chain(nt, True, slot)
            pump(chain, 99)
        if bi == 0:
            inproj_xbc(nt, ALL_PIECES, W_, TB)
        else:
            inproj_xbc(nt, ALL_PIECES, HALO if bi == 1 else 0, 0)
        for q in range(4):
            slot = wpiece([win[:, q * 512:(q + 1) * 512]])
            for i in range(nt):
                psd = nextD()
                for k in range(8):
                    mm(psd, psd.ap(), uT.ap(k * W_ + HALO + i * 128, k * W_ + HALO + (i + 1) * 128),
                       slot.ap(k * 512, (k + 1) * 512), k == 0, k == 7, [uT, slot], inc=(k == 7))
                d = siluz.sub(i * 2048 + q * 512, i * 2048 + (q + 1) * 512)
                act(d.ap(), psd.ap(), AF.Silu, [psd], [d])
        for i in range(nt):
            xB_transposes(i)
            ssd_y(i)
            state_update(i)
        mark("ssd%d" % bi)
        for q in range(2):
            slot = wpiece([win[:, O3 + q * 512:O3 + (q + 1) * 512]])
            for j in range(4):
                ch = q * 4 + j
                wide = nextWide()
                formA(wide, slot, 0, 8, 512, j, uT, W_, W_, [uT])
                d = pT.sub(ch * 544, ch * 544 + W_)
                if bi == 1:
                    act(d.ap(0, HALO), wide.ap(0, HALO), AF.Identity, [wide, pp], [d], scale=ppc(PP_HM))
                    act(d.ap(HALO, W_), wide.ap(HALO, W_), AF.Copy, [wide], [d])
                else:
                    act(d.ap(), wide.ap(0, W_), AF.Copy, [wide], [d])
        if True:
            for ch in range(8):
                g = ch // 2
                src = pT.sub(ch * 544, ch * 544 + W_)
                cur = src
                for s_ in range(g + 1):
                    sh = 1 << s_
                    nb = spp[s_ % 2]
                    lo_ = (2 << s_) - 1
                    tt(nb.ap(lo_, W_), cur.ap(lo_, W_), cur.ap(lo_ - sh, W_ - sh), ALU.add, [cur], [nb])
                    cur = nb
                d = pbT.sub(ch * 512, ch * 512 + TB)
                stt(d.ap(), cur.ap(HALO, W_), 1.0 / (1 << (g + 1)), src.ap(HALO, W_), ALU.mult, ALU.subtract,
                    [cur, src], [d])
                if bi == 1:
                    dd = d.sub(0, 32)
                    tt(m1b.ap(0, 32), cur.ap(HALO, HALO + 32), rows.ap(RW_RC + g * 32, RW_RC + (g + 1) * 32), ALU.mult,
                       [cur, rows], [m1b])
                    tt(dd.ap(), m1b.ap(0, 32), src.ap(HALO, HALO + 32), ALU.subtract, [m1b, src], [dd])
        for g in range(4):
            slot = wpiece([wpg[g * 256:(g + 1) * 256, :]])
            for j in range(2):
                ch = g * 2 + j
                psd = nextD()
                for k in range(2):
                    mm(psd, psd.ap(0, TB), slot.ap(k * 256 + j * 128, k * 256 + (j + 1) * 128),
                       pbT.ap((g * 2 + k) * 512, (g * 2 + k) * 512 + TB), k == 0, k == 1, [slot, pbT], inc=(k == 1))
                d = pgT.sub(ch * 512, ch * 512 + TB)
                act(d.ap(), psd.ap(0, TB), AF.Identity, [psd, pp], [d], scale=ppc(PP_PSC + ch))
        for c in range(8):
            slot1 = wpiece([win[:, O4 + c * 128:O4 + (c + 1) * 128], win[:, O4 + 1024 + c * 128:O4 + 1024 + (c + 1) * 128]])
            slot2 = wpiece([wsso[:, c * 128:(c + 1) * 128], wpo[:, c * 128:(c + 1) * 128]])
            for (off, dst) in ((0, gab), (1024, gbb)):
                psd = nextD()
                for k in range(8):
                    mm(psd, psd.ap(0, TB), slot1.ap(off + k * 128, off + (k + 1) * 128),
                       uT.ap(k * W_ + HALO, k * W_ + W_), k == 0, k == 7, [slot1, uT], inc=(k == 7))
                act(dst.ap(0, TB), psd.ap(0, TB), AF.Sigmoid, [psd], [dst])
            psd = nextD()
            for k in range(16):
                mm(psd, psd.ap(0, TB), slot2.ap(k * 128, (k + 1) * 128), ynT.ap(k * 512, k * 512 + TB),
                   k == 0, k == 15, [slot2, ynT], inc=(k == 15))
            tt(m1b.ap(0, TB), psd.ap(0, TB), gab.ap(0, TB), ALU.mult, [psd, gab], [m1b])
            psd = nextD()
            for k in range(8):
                mm(psd, psd.ap(0, TB), slot2.ap(2048 + k * 128, 2048 + (k + 1) * 128), pgT.ap(k * 512, k * 512 + TB),
                   k == 0, k == 7, [slot2, pgT], inc=(k == 7))
            tt(gbb.ap(0, TB), psd.ap(0, TB), gbb.ap(0, TB), ALU.mult, [psd, gbb], [gbb])
            d = mergedT.sub(c * 512, c * 512 + TB)
            tt(d.ap(), m1b.ap(0, TB), gbb.ap(0, TB), ALU.add, [m1b, gbb], [d])
        mark("merge%d" % bi)
        xres = []
        for q in range(2):
            slot = wpiece([wo[:, q * 512:(q + 1) * 512]])
            for i, tl in enumerate(tiles):
                psd = nextD()
                for k in range(8):
                    mm(psd, psd.ap(), mergedT.ap(k * 512 + i * 128, k * 512 + (i + 1) * 128),
                       slot.ap(k * 512, (k + 1) * 512), k == 0, k == 7, [mergedT, slot], inc=(k == 7))
                t_ = tg[i % 2]
                tt(t_.ap(), psd.ap(), g1bc.ap(q * 512, (q + 1) * 512), ALU.mult, [psd, g1bc], [t_])
                hd = hbuf.sub(i * 1024 + q * 512, i * 1024 + (q + 1) * 512)
                xb_ = xt[(i * 2 + q) % 3]
                dma_in(xb_, lambda xb_=xb_: xb_.ap(0, 512), lambda tl=tl, q=q: x_d[tl * 128:(tl + 1) * 128, q * 512:(q + 1) * 512])
                tt(hd.ap(), t_.ap(), xb_.ap(0, 512), ALU.add, [t_, xb_], [hd])
        I("dve", lambda e, W_=W_: e.tensor_copy(out=vT.ap(0, 8 * W_).rearrange("p (k c) -> p k c", k=8)[:, :, 0:HALO],
                                         in_=vhalo.ap().rearrange("p (k c) -> p k c", k=8)), [vhalo], [vT])
        for i in range(nt):
            hb = hbuf.sub(i * 1024, (i + 1) * 1024)
            norm_tile(lambda hb=hb: hb.ap(), hb, 128, vT, HALO + i * 128, A2, SH2, W_)
        I("dve", lambda e, W_=W_, TB=TB: e.tensor_copy(out=vhalo.ap().rearrange("p (k c) -> p k c", k=8),
                                         in_=vT.ap(0, 8 * W_).rearrange("p (k c) -> p k c", k=8)[:, :, TB:TB + HALO]),
          [vT], [vhalo])
        mark("norm2_%d" % bi)
        if bi == 0:
            continue
        pend = None
        fw_i = 0
        for q in range(6):
            nco = min(512, FF - q * 512)
            if pend is not None:
                pend()
                pend = None
            slot = wpiece([wfi[:, FF + q * 512:FF + q * 512 + nco]])
            slotv = wpiece([wfi[:, q * 512:q * 512 + nco]])
            for j in range(nco // 128):
                ch = q * 4 + j
                wide = WIDE[1 + (fw_i % 2)]
                fw_i += 1
                formA(wide, slot, 0, 8, nco, j, vT, W_, W_, [vT])
                if bi == 1:
                    act(wide.ap(0, HALO), wide.ap(0, HALO), AF.Identity, [wide, pp], [wide.sub(0, HALO)], scale=ppc(PP_HM))
                ga_ = gacc[ch % 2]
                act(ga_.ap(0, TB), wide.ap(HALO, W_), AF.Identity, [wide, pp], [ga_],
                    scale=ppc(PP_FW + ch * 3 + 2), bias=ppc(PP_FB + ch))
                for kk in (1, 2):
                    stt(ga_.ap(0, TB), wide.ap(HALO - kk, W_ - kk), ppc(PP_FW + ch * 3 + 2 - kk), ga_.ap(0, TB),
                        ALU.mult, ALU.add, [wide, pp, ga_], [ga_])

                def part2(ch=ch, ga_=ga_, slotv=slotv, nco=nco, j=j):
                    act(ga_.ap(0, TB), ga_.ap(0, TB), AF.Silu, [ga_], [ga_])
                    psd = nextD()
                    for k in range(8):
                        mm(psd, psd.ap(0, TB), slotv.ap(k * nco + j * 128, k * nco + (j + 1) * 128),
                           vT.ap(k * W_ + HALO, k * W_ + W_), k == 0, k == 7, [slotv, vT], inc=(k == 7))
                    d = hiddenT.sub(ch * 512, ch * 512 + TB)
                    tt(d.ap(), psd.ap(0, TB), ga_.ap(0, TB), ALU.mult, [psd, ga_], [d])
                if pend is not None:
                    pend()
                pend = part2
        pend()
        mark("ffnin%d" % bi)
        for q in range(4):
            slA = wpiece([wfo[0:1408, q * 256:(q + 1) * 256]])
            slB = wpiece([wfo[1408:2816, q * 256:(q + 1) * 256]])
            pst = [D0.sub(0, 256), D1.sub(0, 256), Qb.sub(0, 256), YO.sub(0, 256)]
            for (sl, kbase) in ((slA, 0), (slB, 11)):
                for i in range(nt):
                    for k in range(11):
                        kk = kbase + k
                        mm(pst[i], pst[i].ap(), hiddenT.ap(kk * 512 + i * 128, kk * 512 + (i + 1) * 128),
                           sl.ap(k * 256, (k + 1) * 256), kk == 0, kk == 21, [hiddenT, sl], inc=(k == 10))
            for i in range(nt):
                t_ = tg[i % 2]
                tt(t_.ap(0, 256), pst[i].ap(), g2bc.ap(q * 256, (q + 1) * 256), ALU.mult, [pst[i], g2bc], [t_])
                hd = hbuf.sub(i * 1024 + q * 256, i * 1024 + (q + 1) * 256)
                tt(hd.ap(), hd.ap(), t_.ap(0, 256), ALU.add, [hd, t_], [hd])
        for i, tl in enumerate(tiles):
            hb = hbuf.sub(i * 1024, (i + 1) * 1024)
            ob = outb[i % 2]
            act(ob.ap(), hb.ap(), AF.Square, [hb], [ob, ssq], accum_out=ssq.ap(0, 1))
            act(ssq.ap(1, 2), ssq.ap(0, 1), AF.Sqrt, [ssq], [ssq], scale=1.0 / D, bias=EPS)
            I("dve", lambda e: e.reciprocal(out=ssq.ap(2, 3), in_=ssq.ap(1, 2)), [ssq], [ssq])
            stt(ob.ap(), hb.ap(), ssq.ap(2, 3), rows.ap(RW_GFIN, RW_GFIN + 1024), ALU.mult, ALU.mult,
                [hb, ssq, rows], [ob])
            r0 = (tl - NPRE - 2) * 128
            kb.Dm("sp", lambda r0=r0: y_d[r0:r0 + 128, :], lambda ob=ob: ob.ap(), R=[ob], is_out=True)
        mark("blk%d" % bi)


def _host_tables(inp, core):
    f = np.float32
    pp = np.zeros((128, NPP), f)

    def colmajor(v):
        v = np.asarray(v, f).reshape(-1, 128)
        return v.T

    pp[:, PP_C:PP_C + 8] = colmajor(inp["c"][0])
    b = inp["b_ada"][0]
    pp[:, PP_G1:PP_G1 + 8] = colmajor(inp["g_norm1"][0])
    pp[:, PP_BSH1:PP_BSH1 + 8] = colmajor(b[0:1024])
    pp[:, PP_BSC1:PP_BSC1 + 8] = colmajor(b[1024:2048])
    pp[:, PP_G2:PP_G2 + 8] = colmajor(inp["g_norm2"][0])
    pp[:, PP_BSH2:PP_BSH2 + 8] = colmajor(b[3072:4096])
    pp[:, PP_BSC2:PP_BSC2 + 8] = colmajor(b[4096:5120])
    cw = inp["ssd_conv_w"][0]
    for k in range(4):
        pp[:, PP_CW + k:PP_CW + 96:4] = colmajor(cw[k])
    pp[:, PP_CB:PP_CB + 24] = colmajor(inp["ssd_conv_b"][0])
    fw = inp["ffn_conv_w"][0]
    for k in range(3):
        pp[:, PP_FW + k:PP_FW + 66:3] = colmajor(fw[k])
    pp[:, PP_FB:PP_FB + 22] = colmajor(inp["ffn_conv_b"][0])
    pp[:, PP_GSSD:PP_GSSD + 16] = colmajor(inp["g_ssd_norm"][0])
    pp[:, PP_PSC:PP_PSC + 8] = colmajor(inp["pool_scale"][0])
    pp[:, PP_HM] = 0.0 if core == 0 else 1.0
    for j in range(8):
        pp[:, PP_SEL + j] = 1.0 if j < core else 0.0
    for bb in range(28):
        pp[:, PP_MK + bb] = 1.0 if bb >= 28 - 4 * core else 0.0
        pp[:, PP_MH + bb] = 1.0 if (bb - 1) >= 28 - 4 * core else 0.0
    rows = np.zeros((128, NRW), f)
    rows[:, RW_GFIN:RW_GFIN + 1024] = inp["g_final"][None, :]
    rows[:, RW_DTB:RW_DTB + 32] = inp["ssd_dt_bias"][0][None, :]
    rows[:, RW_ALOG:RW_ALOG + 32] = inp["ssd_a_log"][0][None, :]
    rows[:, RW_DSK:RW_DSK + 32] = inp["ssd_d"][0][None, :]
    for g in range(4):
        w = 1 << (g + 1)
        t = np.arange(32) + core * T
        cnt = np.minimum(t + 1, w).astype(f)
        rows[:, RW_RC + g * 32:RW_RC + (g + 1) * 32] = (1.0 / cnt)[None, :]
    bg = np.zeros((128, 2048), f)
    bg[:, 0:1024] = b[2048:3072][None, :]
    bg[:, 1024:2048] = b[5120:6144][None, :]
    return pp, rows, bg


def kernel(**inputs):
    inp = {k: np.asarray(v) for k, v in inputs.items()}
    if "nc" not in _NC_CACHE:
        _NC_CACHE["nc"] = build_program()
    nc = _NC_CACHE["nc"]
    f = np.float32
    x = np.ascontiguousarray(inp["x"][0], dtype=f)
    PADT = 14336 + 128
    xpad = np.concatenate([np.zeros((PADT, D), f), x], axis=0)
    cst = np.zeros((128, 384), f)
    cst[:, 0:128] = np.eye(128, dtype=f)
    cst[:, 128:256] = np.triu(np.ones((128, 128), f))
    cst[:, 256:384] = 1.0
    shared = {
        "w_ada": np.ascontiguousarray(inp["w_ada"][0], f),
        "w_in": np.ascontiguousarray(inp["w_in"][0], f),
        "w_ssd_out": np.ascontiguousarray(inp["w_ssd_out"][0], f),
        "w_pool_grp": np.ascontiguousarray(inp["w_pool_grp"][0].reshape(1024, 256), f),
        "w_pool_out": np.ascontiguousarray(inp["w_pool_out"][0], f),
        "w_out": np.ascontiguousarray(inp["w_out"][0], f),
        "w_ffn_in": np.ascontiguousarray(inp["w_ffn_in"][0], f),
        "w_ffn_out": np.ascontiguousarray(inp["w_ffn_out"][0], f),
        "cst": cst,
    }
    in_maps = []
    for c in range(NCORES):
        pp, rows, bg = _host_tables(inp, c)
        m = dict(shared)
        m["x"] = np.ascontiguousarray(xpad[c * T:c * T + NTS * 128])
        m["pp"] = pp
        m["rows"] = rows
        m["bgate"] = bg
        in_maps.append(m)
    res = run_bass_kernel_spmd(nc, in_maps, core_ids=list(range(NCORES)))
    out = np.concatenate([np.asarray(r["y"]) for r in res.results], axis=0)
    return out.reshape(1, NCORES * T, D).astype(f)
```

```python
import numpy as np
from contextlib import ExitStack
import concourse.bass as bass
import concourse.mybir as mybir
from concourse.bass_utils import run_bass_kernel_spmd

F32 = mybir.dt.float32
BF16 = mybir.dt.bfloat16
AF = mybir.ActivationFunctionType
ALU = mybir.AluOpType

NCORES = 8
D = 1024
KC = 8
T = 2048
NPRE = 111
NTS = 18 + NPRE
HALO = 32
NH = 32
HD = 64
O1 = 2048
O2 = O1 + 3072
O3 = O2 + 32
O4 = O3 + 1024
FF = 2816
FFC = 22
EPS = 1e-6
SEM_CAP = 12000
_NC_CACHE = {}

PP_C = 0
PP_G1 = 8
PP_BSH1 = 16
PP_BSC1 = 24
PP_G2 = 32
PP_BSH2 = 40
PP_BSC2 = 48
PP_CW = 56
PP_CB = PP_CW + 96
PP_FW = PP_CB + 24
PP_FB = PP_FW + 66
PP_GSSD = PP_FB + 22
PP_PSC = PP_GSSD + 16
PP_HM = PP_PSC + 8
PP_SEL = PP_HM + 1
PP_MK = PP_SEL + 8
PP_MH = PP_MK + 28
NPP = PP_MH + 28
RW_GFIN = 0
RW_DTB = 1024
RW_ALOG = 1056
RW_DSK = 1088
RW_RC = 1120
NRW = RW_RC + 128


class _Stop(Exception):
    pass


KSTOP = None


class Arena:
    def __init__(self, t, ncols):
        self.t = t
        self.ncols = ncols
        self.recs = []
        self.pos = 0
        self.bank = None


class Buf:
    def __init__(self, arena, lo, n):
        self.arena = arena
        self.lo = lo
        self.n = n
        self.hi = lo + n

    def ap(self, a=0, b=None, p=128):
        b = self.n if b is None else b
        return self.arena.t[0:p, self.lo + a:self.lo + b]

    def sub(self, a, b):
        return Buf(self.arena, self.lo + a, b - a)


class BufV(Buf):
    def ap(self, a=0, b=None, p=128):
        b = 2 * self.n if b is None else b
        return self.arena.t[0:p, self.lo:self.hi].bitcast(BF16)[:, a:b]


class KB:
    ENGS = ("pe", "act", "dve", "pool", "sp")

    def __init__(self, nc, es, dry):
        self.nc = nc
        self.dry = dry
        self.q = {e: [] for e in self.ENGS}
        self.cnt = {e: 0 for e in self.ENGS}
        self.semi = {e: 0 for e in self.ENGS}
        self.sems = {}
        self.known = {e: {} for e in self.ENGS}
        self.ndsem = 10
        self.dsems = {}
        self.dcnt = {}
        self.dnext = {"sp": 0, "pool": 0, "act": 0}
        self.out_toks = []
        self.log = {e: [] for e in self.ENGS}
        if not dry:
            for e in self.ENGS:
                self.sems[e] = [es.enter_context(nc.semaphore("s_%s_%d" % (e, i))) for i in range(4)]
            for qn in ("sp", "pool"):
                self.dsems[qn] = [es.enter_context(nc.semaphore("d_%s_%d" % (qn, i))) for i in range(self.ndsem)]
                self.dcnt[qn] = [0] * self.ndsem

    def _wait(self, eng, tok):
        key, val, src = tok
        if src == eng and eng == "pe":
            return
        if self.known[eng].get(key, 0) >= val:
            return
        self.known[eng][key] = val
        sem = self._semobj(key)
        self.log[eng].append(('w', key, val))
        self.q[eng].append(lambda e, s=sem, v=val: e.wait_ge(s, v))

    def _semobj(self, key):
        kind, name, i = key
        if kind == "e":
            return self.sems[name][i]
        return self.dsems[name][i]

    @staticmethod
    def _rng(b):
        bk = b.arena.bank
        if bk is None:
            return b.lo, b.hi
        return (b.lo // bk) * bk, ((b.hi + bk - 1) // bk) * bk

    def _deps(self, eng, R, W):
        toks = []
        for (lst, isw) in ((R, False), (W, True)):
            for b in lst:
                lo, hi = self._rng(b)
                ps = b.arena.bank is not None
                for r in b.arena.recs:
                    if r[0] < hi and lo < r[1]:
                        if isw or r[2] or (ps and r[3][2] != eng):
                            toks.append(r[3])
        for t in toks:
            self._wait(eng, t)

    def _record(self, eng, R, W, tok):
        for (lst, isw) in ((R, False), (W, True)):
            for b in lst:
                a = b.arena
                lo, hi = self._rng(b)
                if isw or a.bank is not None:
                    a.recs = [r for r in a.recs if not (lo <= r[0] and r[1] <= hi)]
                else:
                    a.recs = [r for r in a.recs if not ((not r[2]) and r[3][2] == eng and r[3][0][0] == "e"
                                                        and tok[0][0] == "e" and lo <= r[0] and r[1] <= hi)]
                a.recs.append((lo, hi, isw, tok))

    def I(self, eng, fn, R=(), W=(), inc=True):
        if self.dry:
            return
        self._deps(eng, R, W)
        if inc:
            if self.cnt[eng] >= SEM_CAP:
                self.semi[eng] += 1
                self.cnt[eng] = 0
            self.cnt[eng] += 1
            val = self.cnt[eng]
            si = self.semi[eng]
            sem = self.sems[eng][si]
            self.log[eng].append(('i', ('e', eng, si), 1))
            self.q[eng].append(lambda e, fn=fn, sem=sem: fn(e).then_inc(sem, 1))
        else:
            val = self.cnt[eng] + 1
            si = self.semi[eng]
            assert val <= SEM_CAP
            self.log[eng].append(('n', None, 0))
            self.q[eng].append(lambda e, fn=fn: fn(e))
        tok = (("e", eng, si), val, eng)
        self._record(eng, R, W, tok)

    def Dm(self, qn, out_fn, in_fn, R=(), W=(), is_out=False, **kw):
        if self.dry:
            return
        self._deps(qn, R, W)
        j = self.dnext[qn]
        self.dnext[qn] = (j + 1) % self.ndsem
        key = ("d", qn, j)
        if self.dcnt[qn][j] > 0:
            self._wait(qn, (key, 16 * self.dcnt[qn][j], "dma"))
        self.dcnt[qn][j] += 1
        val = 16 * self.dcnt[qn][j]
        sem = self.dsems[qn][j]
        self.log[qn].append(('i', key, 16))
        self.q[qn].append(lambda e, o=out_fn, i=in_fn, sem=sem, kw=kw: e.dma_start(out=o(), in_=i(), **kw).then_inc(sem, 16))
        tok = (key, val, "dma")
        self._record("dma", R, W, tok)
        if is_out:
            self.out_toks.append(tok)

    def finish(self):
        if self.dry:
            return
        for t in self.out_toks:
            self._wait("sp", t)


def build_program():
    nc = bass.Bass("TRN2", target_bir_lowering=False)
    dr = {}

    def din(name, shape):
        dr[name] = nc.dram_tensor(name, shape, F32, kind="ExternalInput").ap()
        return dr[name]

    x_d = din("x", [NTS * 128, D])
    wada_d = din("w_ada", [D, 6 * D])
    win_d = din("w_in", [D, 8224])
    wsso_d = din("w_ssd_out", [2048, D])
    wpg_d = din("w_pool_grp", [1024, 256])
    wpo_d = din("w_pool_out", [D, D])
    wo_d = din("w_out", [D, D])
    wfi_d = din("w_ffn_in", [D, 2 * FF])
    wfo_d = din("w_ffn_out", [FF, D])
    pp_d = din("pp", [128, NPP])
    rw_d = din("rows", [128, NRW])
    bg_d = din("bgate", [128, 2048])
    cst_d = din("cst", [128, 384])
    y_d = nc.dram_tensor("y", [T, D], F32, kind="ExternalOutput").ap()
    cc_in = nc.dram_tensor("cc_in", [128, 2080], F32, kind="Internal").ap()
    cc_out = nc.dram_tensor("cc_out", [NCORES * 128, 2080], F32, kind="Internal").ap()

    A16N = 3472 + 12544 + 36608
    A32N = 12160 + 8256
    with ExitStack() as es:
        w16_t = es.enter_context(nc.sbuf_tensor("w16", [128, 3 * 4096], BF16))
        a16_t = es.enter_context(nc.sbuf_tensor("a16", [128, A16N], BF16))
        a32_t = es.enter_context(nc.sbuf_tensor("a32", [128, A32N], F32))
        ps_t = es.enter_context(nc.psum_tensor("ps", [128, 3584], F32))
        pt_t = es.enter_context(nc.psum_tensor("pt", [128, 1024], BF16))
        kb_real = KB(nc, es, False)
        plan = []
        for dry in (True, False):
            kb = KB(nc, es, True) if dry else kb_real
            arenas = dict(w16=Arena(w16_t, 3 * 4096), a16=Arena(a16_t, A16N), a32=Arena(a32_t, A32N),
                          ps=Arena(ps_t, 3584), pt=Arena(pt_t, 1024), cc=Arena(None, 4))
            try:
                arenas['ps'].bank = 512
                arenas['pt'].bank = 1024
                emit(nc, kb, arenas, dr, y_d, cc_in, cc_out, plan, dry)
            except _Stop:
                pass
        kb = kb_real
        kb.finish()
        _NC_CACHE['kb'] = kb
        block = es.enter_context(nc.Block())

        @block.sync
        def _(e):
            for f in kb.q["sp"]:
                f(e)

        @block.gpsimd
        def _(e):
            for f in kb.q["pool"]:
                f(e)

        @block.tensor
        def _(e):
            for f in kb.q["pe"]:
                f(e)

        @block.vector
        def _(e):
            for f in kb.q["dve"]:
                f(e)

        @block.scalar
        def _(e):
            for f in kb.q["act"]:
                f(e)
    return nc


def emit(nc, kb, AR, dr, y_d, cc_in, cc_out, plan, dry):
    I = kb.I
    x_d = dr["x"]

    def mark(name):
        if KSTOP is not None and name == KSTOP:
            raise _Stop()

    def alloc(an, n):
        a = AR[an]
        b = Buf(a, a.pos, n)
        a.pos += n
        assert a.pos <= a.ncols, (an, a.pos, a.ncols)
        return b

    wslots = [alloc("w16", 4096) for _ in range(3)]
    S_bf = alloc("a16", 2048)
    ident_bf = alloc("a16", 128)
    vhalo = alloc("a16", 256)
    cond_bf = alloc("a16", 16)
    cond_bc = alloc("a16", 1024)
    AR["a16"].pos = 3472
    uT = alloc("a16", 8 * 544)
    ynT = alloc("a16", 16 * 512)
    X16 = AR["a16"].pos
    xBCc = alloc("a16", 24 * 512)
    siluz = alloc("a16", 4 * 2048)
    XS = [dict(x_tm=alloc("a16", 2048), xw=alloc("a16", 2048), xD=alloc("a16", 2048), B_tm=alloc("a16", 512))
          for _ in range(2)]
    Mbs = [alloc("a16", 1024) for _ in range(2)]
    ynb = alloc("a16", 512)
    AR["a16"].pos = X16
    pbT = alloc("a16", 8 * 512)
    pgT = alloc("a16", 8 * 512)
    mergedT = alloc("a16", 8 * 512)
    hn = alloc("a16", 1024)
    hn2 = alloc("a16", 1024)
    hnb = [hn, hn2]
    vT = alloc("a16", 8 * 544)
    hiddenT = alloc("a16", 22 * 512)

    S32 = alloc("a32", 2048)
    g1bc = alloc("a32", 1024)
    g2bc = alloc("a32", 1024)
    rows = alloc("a32", NRW)
    cst = alloc("a32", 384)
    pp = alloc("a32", NPP)
    modp = alloc("a32", 32)
    a_bc = alloc("a32", 32)
    totsum = alloc("a32", 32)
    Dj = alloc("a32", 32)
    smA = [alloc("a32", 128) for _ in range(11)]
    smB = [alloc("a32", 128) for _ in range(11)]
    smsets = [smA, smB]
    sm = smA
    ssq = alloc("a32", 16)
    xt = [alloc("a32", 1024) for _ in range(3)]
    assert AR["a32"].pos <= 12160, AR["a32"].pos
    AR["a32"].pos = 12160
    Y32 = 12160
    CBm = alloc("a32", 512)
    Rbs = [alloc("a32", 1024) for _ in range(2)]
    Ebs = [alloc("a32", 1024) for _ in range(2)]
    sqj = alloc("a32", 512)
    tb = [alloc("a32", 512) for _ in range(2)]
    cacc = [alloc("a32", 544) for _ in range(2)]
    AR["a32"].pos = Y32
    Gx = [alloc("a32", 2080) for _ in range(2)]
    AR["a32"].pos = Y32
    bgt = alloc("a32", 2048)
    AR["a32"].pos = Y32
    pT = alloc("a32", 8 * 544)
    spp = [alloc("a32", 544) for _ in range(2)]
    gab = alloc("a32", 512)
    gbb = alloc("a32", 512)
    m1b = alloc("a32", 512)
    AR["a32"].pos = Y32
    hbuf = alloc("a32", 4096)
    tg = [alloc("a32", 512) for _ in range(2)]
    gacc = [alloc("a32", 544) for _ in range(2)]
    outb = [alloc("a32", 1024) for _ in range(2)]

    PS = AR["ps"]
    PT = Buf(AR["pt"], 0, 1024)
    D0 = Buf(PS, 0, 512)
    D1 = Buf(PS, 512, 512)
    Qb = Buf(PS, 1024, 512)
    ABb = Buf(PS, 1536, 1024)
    YD = Buf(PS, 2560, 512)
    YO = Buf(PS, 3072, 512)
    WIDE = [Buf(PS, 0, 1024), Buf(PS, 1024, 1024), Buf(PS, 2048, 1024)]
    PT2 = BufV(PS, 2560, 512)
    PTS = [PT, PT2]
    st = dict(d=0, wide=0, wi=0, xt=0, pt=0, pt2=False, sm=None)

    def nextPT():
        if not st["pt2"]:
            return PT
        st["pt"] ^= 1
        return PTS[st["pt"]]

    def nextD():
        st["d"] ^= 1
        return (D0, D1)[st["d"]]

    def nextWide():
        st["wide"] = (st["wide"] + 1) % 3
        return WIDE[st["wide"]]

    ident32 = cst.sub(0, 128)
    tri32 = cst.sub(128, 256)
    ones32 = cst.sub(256, 384)

    def ppc(col, p=128):
        return pp.ap(col, col + 1, p)

    def wpiece(parts):
        idx = st["wi"]
        st["wi"] += 1
        if dry:
            plan.append(parts)
            return wslots[idx % 3]

        def issue(k):
            if k >= len(plan) or k < st.get("issued", 0):
                return
            st["issued"] = k + 1
            slot = wslots[k % 3]
            off = 0
            for src in plan[k]:
                rows_, nco = src.shape
                kc = rows_ // 128
                dst = slot.sub(off, off + kc * nco)
                kb.Dm("pool",
                      (lambda dst=dst, kc=kc: dst.ap().rearrange("p (k c) -> p k c", k=kc)),
                      (lambda src=src: src.rearrange("(k p) c -> p k c", p=128)),
                      W=[dst])
                off += kc * nco
        for k in range(st.get("issued", 0), idx + 2):
            issue(k)
        return wslots[idx % 3]

    def act(out, in_, func, R, W, **kw):
        I("act", lambda e: e.activation(out=out, in_=in_, func=func, **kw), R, W)

    def tt(out, in0, in1, op, R, W, eng="dve"):
        I(eng, lambda e: e.tensor_tensor(out=out, in0=in0, in1=in1, op=op), R, W)

    def ts(out, in0, s1, op0, R, W, s2=None, op1=None, eng="dve"):
        if op1 is None:
            I(eng, lambda e: e.tensor_scalar(out=out, in0=in0, scalar1=s1, scalar2=None, op0=op0), R, W)
        else:
            I(eng, lambda e: e.tensor_scalar(out=out, in0=in0, scalar1=s1, scalar2=s2, op0=op0, op1=op1), R, W)

    def stt(out, in0, sc, in1, op0, op1, R, W):
        I("dve", lambda e: e.scalar_tensor_tensor(out=out, in0=in0, scalar=sc, in1=in1, op0=op0, op1=op1), R, W)

    def mm(outb_, out, lhsT, rhs, start, stop, R, inc):
        I("pe", lambda e: e.matmul(out, lhsT=lhsT, rhs=rhs, start=start, stop=stop), R, [outb_], inc=inc)

    def tr(outb_, out, in_, idn, R, inc=True):
        I("pe", lambda e: e.transpose(out, in_, idn), R, [outb_], inc=inc)

    def dma_in(dstb, dst_fn, src_fn):
        kb.Dm("sp", dst_fn, src_fn, W=[dstb])

    dma_in(cst, lambda: cst.ap(), lambda: dr["cst"][:, :])
    dma_in(pp, lambda: pp.ap(), lambda: dr["pp"][:, :])
    dma_in(rows, lambda: rows.ap(), lambda: dr["rows"][:, :])
    dma_in(bgt, lambda: bgt.ap(), lambda: dr["bgate"][:, :])
    I("dve", lambda e: e.tensor_copy(out=ident_bf.ap(), in_=ident32.ap()), [ident32], [ident_bf])
    act(sm[0].ap(0, 8), pp.ap(PP_C, PP_C + 8), AF.Silu, [pp], [sm[0]])
    I("dve", lambda e: e.tensor_copy(out=cond_bf.ap(0, 8), in_=sm[0].ap(0, 8)), [sm[0]], [cond_bf])
    for k in range(8):
        I("dve", lambda e, k=k: e.tensor_copy(out=cond_bc.ap(k * 128, (k + 1) * 128),
                                              in_=sm[0].ap(k, k + 1).broadcast_to([128, 128])),
          [sm[0]], [cond_bc.sub(k * 128, (k + 1) * 128)])
    act(a_bc.ap(), rows.ap(RW_ALOG, RW_ALOG + 32), AF.Exp, [rows], [a_bc])
    ts(a_bc.ap(), a_bc.ap(), -1.0, ALU.mult, [a_bc], [a_bc])
    wada = dr["w_ada"]
    for (c0, dstcol, bcol, gcol, is_scale) in ((0, 8, PP_BSH1, None, False), (1024, 0, PP_BSC1, PP_G1, True),
                                               (3072, 24, PP_BSH2, None, False), (4096, 16, PP_BSC2, PP_G2, True)):
        psd = nextD()
        for half in range(2):
            slot = wpiece([wada[:, c0 + half * 512:c0 + (half + 1) * 512]])
            for jj in range(4):
                j = half * 4 + jj
                for k in range(8):
                    mm(psd, psd.ap(j, j + 1), slot.ap(k * 512 + jj * 128, k * 512 + (jj + 1) * 128),
                       cond_bf.ap(k, k + 1), k == 0, k == 7, [slot, cond_bf], inc=(k == 7))
        d = modp.sub(dstcol, dstcol + 8)
        tt(d.ap(), psd.ap(0, 8), pp.ap(bcol, bcol + 8), ALU.add, [psd, pp], [d])
        if is_scale:
            ts(d.ap(), d.ap(), 1.0, ALU.add, [d], [d])
            tt(d.ap(), d.ap(), pp.ap(gcol, gcol + 8), ALU.mult, [d, pp], [d])
    for (c0, gb_, boff) in ((2048, g1bc, 0), (5120, g2bc, 1024)):
        for half in range(2):
            slot = wpiece([wada[:, c0 + half * 512:c0 + (half + 1) * 512]])
            psd = nextD()
            for k in range(8):
                mm(psd, psd.ap(), cond_bc.ap(k * 128, (k + 1) * 128), slot.ap(k * 512, (k + 1) * 512),
                   k == 0, k == 7, [slot, cond_bc], inc=(k == 7))
            d = gb_.sub(half * 512, (half + 1) * 512)
            tt(d.ap(), psd.ap(), bgt.ap(boff + half * 512, boff + (half + 1) * 512), ALU.add, [psd, bgt], [d])

    A1, SH1, A2, SH2 = 0, 8, 16, 24
    st["sm"] = smA
    mark("prologue")

    def pump(ch, n=1):
        for _ in range(n):
            if ch:
                ch.pop(0)()

    def norm_A(src_ap_fn, srcb, np_, slot_):
        hb_ = hnb[slot_]
        sq_ = ssq.sub(8 + slot_ * 4, 12 + slot_ * 4)
        I("dve", lambda e: e.scalar_tensor_tensor(out=hb_.ap(0, 1024, np_), in0=src_ap_fn(), scalar=1.0, in1=src_ap_fn(),
                                                  op0=ALU.mult, op1=ALU.mult, accum_out=sq_.ap(0, 1, np_)),
          [srcb], [hb_, sq_])
        act(sq_.ap(1, 2, np_), sq_.ap(0, 1, np_), AF.Sqrt, [sq_], [sq_], scale=1.0 / D, bias=EPS)
        I("dve", lambda e: e.reciprocal(out=sq_.ap(2, 3, np_), in_=sq_.ap(1, 2, np_)), [sq_], [sq_])
        ts(hb_.ap(0, 1024, np_), src_ap_fn(), sq_.ap(2, 3, np_), ALU.mult, [srcb, sq_], [hb_])

    def norm_B(np_, slot_, dstT, dcol, acol, shcol, width):
        hb_ = hnb[slot_]
        for k in range(8):
            P_ = PT if k < 4 else PT2
            kk = k % 4
            tr(P_, P_.ap(kk * 128, kk * 128 + np_), hb_.ap(k * 128, (k + 1) * 128, np_), ident_bf.ap(0, np_, np_),
               [hb_, ident_bf], inc=(kk == 3))
        for k in range(4):
            d = dstT.sub(k * width + dcol, k * width + dcol + np_)
            act(d.ap(), PT.ap(k * 128, k * 128 + np_), AF.Identity, [PT, modp], [d],
                scale=modp.ap(acol + k, acol + k + 1), bias=modp.ap(shcol + k, shcol + k + 1))
            k2 = k + 4
            d2 = dstT.sub(k2 * width + dcol, k2 * width + dcol + np_)
            ts(d2.ap(), PT2.ap(k * 128, k * 128 + np_), modp.ap(acol + k2, acol + k2 + 1), ALU.mult, [PT2, modp], [d2],
               s2=modp.ap(shcol + k2, shcol + k2 + 1), op1=ALU.add)

    def norm_tile(src_ap_fn, srcb, np_, dstT, dcol, acol, shcol, width):
        norm_A(src_ap_fn, srcb, np_, 0)
        norm_B(np_, 0, dstT, dcol, acol, shcol, width)

    def load_x(row0, np_):
        b = xt[st["xt"]]
        st["xt"] = (st["xt"] + 1) % 3
        dma_in(b, lambda: b.ap(0, 1024, np_), lambda: x_d[row0:row0 + np_, :])
        return b

    def stage_u_thunks(tiles, width):
        t0 = tiles[0]
        items = [(t0 * 128 - HALO, HALO, 0)] + [(tl * 128, 128, HALO + i * 128) for i, tl in enumerate(tiles)]
        bufs_ = []
        th = []

        def first():
            bufs_.append(load_x(items[0][0], items[0][1]))
            if len(items) > 1:
                bufs_.append(load_x(items[1][0], items[1][1]))
            norm_A((lambda b=bufs_[0], n=items[0][1]: b.ap(0, 1024, n)), bufs_[0], items[0][1], 0)
        th.append(first)
        for j, (row0, np_, dcol) in enumerate(items):
            def step(j=j, np_=np_, dcol=dcol):
                if j + 2 < len(items):
                    bufs_.append(load_x(items[j + 2][0], items[j + 2][1]))
                if j + 1 < len(items):
                    nb_ = bufs_[j + 1]
                    norm_A((lambda b=nb_, n=items[j + 1][1]: b.ap(0, 1024, n)), nb_, items[j + 1][1], (j + 1) % 2)
                norm_B(np_, j % 2, uT, dcol, A1, SH1, width)
            th.append(step)
        return th

    def stage_u(tiles, width):
        pump(stage_u_thunks(tiles, width), 99)

    def formA(wide, slot, soff, kc, ncol_w, j, rhsT, width, ncols, rbufs):
        segs = [(0, min(512, ncols))]
        if ncols > 512:
            segs.append((512, ncols))
        for (a, b) in segs:
            for k in range(kc):
                mm(wide.sub(a, b), wide.ap(a, b),
                   slot.ap(soff + k * ncol_w + j * 128, soff + k * ncol_w + (j + 1) * 128),
                   rhsT.ap(k * width + a, k * width + b), k == 0, k == kc - 1, [slot] + rbufs, inc=(k == kc - 1))

    def conv_p1(wide, ch, TB, premask, hcol=None):
        W_ = HALO + TB
        if hcol is not None:
            act(wide.ap(0, HALO), wide.ap(0, HALO), AF.Identity, [wide, pp], [wide.sub(0, HALO)], scale=ppc(hcol))
        if premask > 0:
            ts(wide.ap(0, premask), wide.ap(0, premask), ppc(PP_HM), ALU.mult, [wide, pp], [wide.sub(0, premask)])
        ca = cacc[ch % 2]
        act(ca.ap(0, TB), wide.ap(HALO, W_), AF.Identity, [wide, pp], [ca],
            scale=ppc(PP_CW + ch * 4 + 3), bias=ppc(PP_CB + ch))
        for kk in (1, 2, 3):
            stt(ca.ap(0, TB), wide.ap(HALO - kk, W_ - kk), ppc(PP_CW + ch * 4 + 3 - kk), ca.ap(0, TB),
                ALU.mult, ALU.add, [wide, pp, ca], [ca])

    def conv_p2(ch, TB, postmask):
        ca = cacc[ch % 2]
        d = xBCc.sub(ch * 512, ch * 512 + TB)
        act(d.ap(), ca.ap(0, TB), AF.Silu, [ca], [d])
        if postmask > 0:
            dd = xBCc.sub(ch * 512, ch * 512 + postmask)
            ts(dd.ap(), dd.ap(), ppc(PP_HM), ALU.mult, [dd, pp], [dd])

    win = dr["w_in"]

    def inproj_xbc(nt, chunks_pieces, premask, postmask, hcol=None, chain=None):
        TB = nt * 128
        W_ = HALO + TB
        prev = None
        for (c0, ch0) in chunks_pieces:
            slot = wpiece([win[:, c0:c0 + 512]])
            for j in range(4):
                wide = nextWide()
                formA(wide, slot, 0, 8, 512, j, uT, W_, W_, [uT])
                conv_p1(wide, ch0 + j, TB, premask, hcol)
                if prev is not None:
                    conv_p2(prev, TB, postmask)
                prev = ch0 + j
        conv_p2(prev, TB, postmask)

    def dt_chain(nt, full, slot, mkcol=None, blocklevel=False, smset=None):
        smset = st["sm"] if smset is None else smset
        raw, m_, na, da, lndt, acum, bias_s, w2, expA, dec, tmp = [b_.sub(0, nt * 32) for b_ in smset[0:11]]
        W_ = HALO + nt * 128
        n32 = nt * 32
        ch = []
        hold = {}

        def v3(b_):
            return b_.ap().rearrange("p (t h) -> p t h", t=nt)

        def s1():
            hold["psd"] = YO.sub(0, 128)
            psd = hold["psd"]
            for i in range(nt):
                for k in range(8):
                    mm(psd, psd.ap(i * 32, (i + 1) * 32), uT.ap(k * W_ + HALO + i * 128, k * W_ + HALO + (i + 1) * 128),
                       slot.ap(k * 32, (k + 1) * 32), k == 0, k == 7, [uT, slot], inc=(k == 7))
        ch.append(s1)
        ch.append(lambda: tt(v3(raw), hold["psd"].ap(0, n32).rearrange("p (t h) -> p t h", t=nt),
                             rows.ap(RW_DTB, RW_DTB + 32).unsqueeze(1).broadcast_to([128, nt, 32]), ALU.add,
                             [hold["psd"], rows], [raw]))
        ch.append(lambda: ts(m_.ap(), raw.ap(), 0.0, ALU.max, [raw], [m_]))
        ch.append(lambda: ts(na.ap(), raw.ap(), -1.0, ALU.mult, [raw], [na]))
        ch.append(lambda: tt(na.ap(), na.ap(), raw.ap(), ALU.min, [na, raw], [na]))
        ch.append(lambda: act(na.ap(), na.ap(), AF.Exp, [na], [na]))
        ch.append(lambda: act(na.ap(), na.ap(), AF.Ln, [na], [na], bias=1.0))
        ch.append(lambda: tt(raw.ap(), m_.ap(), na.ap(), ALU.add, [m_, na], [raw]))
        ch.append(lambda: tt(v3(da), v3(raw), a_bc.ap().unsqueeze(1).broadcast_to([128, nt, 32]), ALU.mult,
                             [raw, a_bc], [da]))
        ch.append(lambda: act(lndt.ap(), raw.ap(), AF.Ln, [raw], [lndt]))

        def s2():
            hold["psa"] = YO.sub(128, 384)
            psa = hold["psa"]
            for i in range(nt):
                mm(psa, psa.ap(i * 32, (i + 1) * 32), tri32.ap(), da.ap(i * 32, (i + 1) * 32), True, True,
                   [tri32, da], inc=True)
                mm(psa, psa.ap(128 + i * 32, 128 + (i + 1) * 32), ones32.ap(), da.ap(i * 32, (i + 1) * 32), True, True,
                   [ones32, da], inc=True)
        ch.append(s2)
        ch.append(lambda: I("dve", lambda e: e.tensor_copy(out=acum.ap(), in_=hold["psa"].ap(0, n32)),
                            [hold["psa"]], [acum]))
        ch.append(lambda: tt(bias_s.ap(), acum.ap(), lndt.ap(), ALU.subtract, [acum, lndt], [bias_s]))
        ch.append(lambda: tt(tmp.ap(), hold["psa"].ap(128, 128 + n32), bias_s.ap(), ALU.subtract,
                             [hold["psa"], bias_s], [tmp]))
        if blocklevel:
            def sfx():
                psa = hold["psa"]
                run = smset[8].sub(0, 32)
                I("dve", lambda e: e.tensor_copy(out=run.ap(), in_=psa.ap(128 + (nt - 1) * 32, 128 + nt * 32)), [psa], [run])
                for i in range(nt - 2, -1, -1):
                    ti = tmp.sub(i * 32, (i + 1) * 32)
                    tt(ti.ap(), ti.ap(), run.ap(), ALU.add, [ti, run], [ti])
                    tt(run.ap(), run.ap(), psa.ap(128 + i * 32, 128 + (i + 1) * 32), ALU.add, [run, psa], [run])
                act(dec.ap(0, 32), run.ap(), AF.Exp, [run], [dec])
            ch.append(sfx)
            ch.append(lambda: act(w2.ap(), tmp.ap(), AF.Exp, [tmp], [w2]))
        else:
            ch.append(lambda: act(w2.ap(), tmp.ap(), AF.Exp, [tmp], [w2]))
            ch.append(lambda: act(dec.ap(), hold["psa"].ap(128, 128 + n32), AF.Exp, [hold["psa"]], [dec]))
        if full:
            ch.append(lambda: act(expA.ap(), acum.ap(), AF.Exp, [acum], [expA]))
        if mkcol is not None:
            ch.append(lambda: ts(w2.ap(), w2.ap(), ppc(mkcol), ALU.mult, [w2, pp], [w2]))
        return ch

    PT3 = BufV(PS, 512, 512)

    def xB_parts(i, mkcol=None, bank=None):
        raw, m_, na, da, lndt, acum, bias_s, w2, expA, dec, tmp = [b_.sub(i * 32, (i + 1) * 32) for b_ in st["sm"][0:11]]
        X = XS[i % 2]
        x_tm, xw, xD, B_tm = X["x_tm"], X["xw"], X["xD"], X["B_tm"]

        def xr(r):
            PT = bank if bank is not None else nextPT()
            for j in range(8):
                ch = r * 8 + j
                tr(PT, PT.ap(j * 128, (j + 1) * 128), xBCc.ap(ch * 512 + i * 128, ch * 512 + (i + 1) * 128),
                   ident_bf.ap(), [xBCc, ident_bf], inc=(j == 7))
            if mkcol is None:
                d = x_tm.sub(r * 1024, (r + 1) * 1024)
                I("act", lambda e, d=d, PT=PT: e.copy(out=d.ap(), in_=PT.ap()), [PT], [d])
            d2 = xw.sub(r * 1024, (r + 1) * 1024)
            tt(d2.ap().rearrange("p (h c) -> p h c", h=16), PT.ap().rearrange("p (h c) -> p h c", h=16),
               w2.ap(r * 16, (r + 1) * 16).unsqueeze(2).broadcast_to([128, 16, 64]), ALU.mult, [PT, w2], [d2])
            d3 = xD.sub(r * 1024, (r + 1) * 1024)
            if mkcol is None:
                tt(d3.ap().rearrange("p (h c) -> p h c", h=16), PT.ap().rearrange("p (h c) -> p h c", h=16),
                   rows.ap(RW_DSK + r * 16, RW_DSK + (r + 1) * 16).unsqueeze(2).broadcast_to([128, 16, 64]),
                   ALU.mult, [PT, rows], [d3])

        def xb_():
            PT = bank if bank is not None else nextPT()
            for j in range(4):
                ch = 16 + j
                tr(PT, PT.ap(j * 128, (j + 1) * 128), xBCc.ap(ch * 512 + i * 128, ch * 512 + (i + 1) * 128),
                   ident_bf.ap(), [xBCc, ident_bf], inc=(j == 3))
            I("act", lambda e, PT=PT: e.copy(out=B_tm.ap(), in_=PT.ap(0, 512)), [PT], [B_tm])
        return [lambda: xr(0), lambda: xr(1), xb_]

    def xB_transposes(i, mkcol=None):
        pump(xB_parts(i, mkcol), 99)

    def state_update(i):
        dec = st["sm"][9].sub(i * 32, (i + 1) * 32)
        B_tm, xw = XS[i % 2]["B_tm"], XS[i % 2]["xw"]
        for g in range(4):
            psd = D0
            mm(psd, psd.ap(), B_tm.ap(g * 128, (g + 1) * 128), xw.ap(g * 512, (g + 1) * 512), True, True,
               [B_tm, xw], inc=True)
            sg = S32.sub(g * 512, (g + 1) * 512)
            tt(sg.ap().rearrange("p (h c) -> p h c", h=8), sg.ap().rearrange("p (h c) -> p h c", h=8),
               dec.ap(g * 8, (g + 1) * 8).unsqueeze(2).broadcast_to([128, 8, 64]), ALU.mult, [sg, dec], [sg])
            tt(sg.ap(), sg.ap(), psd.ap(), ALU.add, [sg, psd], [sg])
            sb = S_bf.sub(g * 512, (g + 1) * 512)
            I("act", lambda e, sb=sb, sg=sg: e.copy(out=sb.ap(), in_=sg.ap()), [sg], [sb])

    SACC = [D0, D1, Qb, ABb.sub(0, 512)]

    def state_accum(i, nt):
        B_tm, xw = XS[i % 2]["B_tm"], XS[i % 2]["xw"]
        for g in range(4):
            mm(SACC[g], SACC[g].ap(), B_tm.ap(g * 128, (g + 1) * 128), xw.ap(g * 512, (g + 1) * 512), i == 0, i == nt - 1,
               [B_tm, xw], inc=True)

    def state_finish():
        dec = st["sm"][9].sub(0, 32)
        for g in range(4):
            sg = S32.sub(g * 512, (g + 1) * 512)
            tt(sg.ap().rearrange("p (h c) -> p h c", h=8), sg.ap().rearrange("p (h c) -> p h c", h=8),
               dec.ap(g * 8, (g + 1) * 8).unsqueeze(2).broadcast_to([128, 8, 64]), ALU.mult, [sg, dec], [sg])
            tt(sg.ap(), sg.ap(), SACC[g].ap(), ALU.add, [sg, SACC[g]], [sg])

    def ssd_y(i, nxt_parts=None):
        raw, m_, na, da, lndt, acum, bias_s, w2, expA, dec, tmp = [b_.sub(i * 32, (i + 1) * 32) for b_ in st["sm"][0:11]]
        c0, c1 = i * 128, (i + 1) * 128
        x_tm, xD = XS[i % 2]["x_tm"], XS[i % 2]["xD"]
        nxt_parts = [] if nxt_parts is None else nxt_parts
        for g in range(4):
            mm(Qb, Qb.ap(g * 128, (g + 1) * 128), xBCc.ap((16 + g) * 512 + c0, (16 + g) * 512 + c1),
               xBCc.ap((20 + g) * 512 + c0, (20 + g) * 512 + c1), True, True, [xBCc], inc=(g == 3))
        tt(CBm.ap().rearrange("p (g c) -> p g c", g=4), Qb.ap().rearrange("p (g c) -> p g c", g=4),
           tri32.ap().unsqueeze(1).broadcast_to([128, 4, 128]), ALU.mult, [Qb, tri32], [CBm])

        def R_(g):
            Rb = Rbs[g % 2]
            tt(Rb.ap().rearrange("p (h c) -> p h c", h=8),
               acum.ap(g * 8, (g + 1) * 8).unsqueeze(2).broadcast_to([128, 8, 128]),
               ident32.ap().unsqueeze(1).broadcast_to([128, 8, 128]), ALU.mult, [acum, ident32], [Rb])

        def AB_(g):
            Rb = Rbs[g % 2]
            for hh in range(2):
                mm(ABb.sub(hh * 512, (hh + 1) * 512), ABb.ap(hh * 512, (hh + 1) * 512), ones32.ap(),
                   Rb.ap(hh * 512, (hh + 1) * 512), True, True, [ones32, Rb], inc=True)

        def SUB_(g):
            Eb = Ebs[g % 2]
            tt(Eb.ap().rearrange("p (h c) -> p h c", h=8), ABb.ap().rearrange("p (h c) -> p h c", h=8),
               bias_s.ap(g * 8, (g + 1) * 8).unsqueeze(2).broadcast_to([128, 8, 128]), ALU.subtract,
               [ABb, bias_s], [Eb])

        def EXP_(g):
            Eb = Ebs[g % 2]
            act(Eb.ap(), Eb.ap(), AF.Exp, [Eb], [Eb])

        def M_(g):
            Eb, Mb = Ebs[g % 2], Mbs[g % 2]
            stt(Mb.ap().rearrange("p (h c) -> p h c", h=8), Eb.ap().rearrange("p (h c) -> p h c", h=8), 1e30,
                CBm.ap(g * 128, (g + 1) * 128).unsqueeze(1).broadcast_to([128, 8, 128]), ALU.min, ALU.mult,
                [Eb, CBm], [Mb])

        def Y_(g):
            Mb = Mbs[g % 2]
            mm(YD, YD.ap(), ident_bf.ap(), xD.ap(g * 512, (g + 1) * 512), True, False, [ident_bf, xD], inc=False)
            for h in range(8):
                hh = g * 8 + h
                mm(YD, YD.ap(h * 64, (h + 1) * 64), Mb.ap(h * 128, (h + 1) * 128), x_tm.ap(hh * 64, (hh + 1) * 64),
                   False, h == 7, [Mb, x_tm], inc=(h == 7))
            mm(YO, YO.ap(), xBCc.ap((20 + g) * 512 + c0, (20 + g) * 512 + c1), S_bf.ap(g * 512, (g + 1) * 512),
               True, True, [xBCc, S_bf], inc=True)

        def T_(g):
            t_ = tb[g % 2]
            tt(t_.ap().rearrange("p (h c) -> p h c", h=8), YO.ap().rearrange("p (h c) -> p h c", h=8),
               expA.ap(g * 8, (g + 1) * 8).unsqueeze(2).broadcast_to([128, 8, 64]), ALU.mult, [YO, expA], [t_])
            tt(t_.ap(), YD.ap(), t_.ap(), ALU.add, [YD, t_], [t_])
            sz = siluz.sub(i * 2048 + g * 512, i * 2048 + (g + 1) * 512)
            tt(t_.ap(), t_.ap(), sz.ap(), ALU.mult, [t_, sz], [t_])

        def N_(g):
            t_ = tb[g % 2]
            sq_ = ssq.sub(4 + (g % 2) * 4, 8 + (g % 2) * 4) if False else ssq.sub(4, 8)
            act(sqj.ap(), t_.ap(), AF.Square, [t_], [sqj, sq_], accum_out=sq_.ap(0, 1))
            act(sq_.ap(1, 2), sq_.ap(0, 1), AF.Ln, [sq_], [sq_], scale=1.0 / 512, bias=EPS)
            act(sq_.ap(2, 3), sq_.ap(1, 2), AF.Exp, [sq_], [sq_], scale=-0.5)

        def YN_(g):
            t_ = tb[g % 2]
            sq_ = ssq.sub(4, 8)
            ts(ynb.ap(), t_.ap(), sq_.ap(2, 3), ALU.mult, [t_, sq_], [ynb])

        def TR_(g):
            for j in range(4):
                tr(PT, PT.ap(j * 128, (j + 1) * 128), ynb.ap(j * 128, (j + 1) * 128), ident_bf.ap(),
                   [ynb, ident_bf], inc=(j == 3))
            for j in range(4):
                ch = g * 4 + j
                d = ynT.sub(ch * 512 + c0, ch * 512 + c1)
                act(d.ap(), PT.ap(j * 128, (j + 1) * 128), AF.Identity, [PT, pp], [d], scale=ppc(PP_GSSD + ch))

        R_(0)
        AB_(0)
        SUB_(0)
        EXP_(0)
        M_(0)
        for g in range(4):
            nx = g + 1 < 4
            if nx:
                R_(g + 1)
            Y_(g)
            if nx:
                AB_(g + 1)
            T_(g)
            if nx:
                SUB_(g + 1)
            N_(g)
            if nx:
                EXP_(g + 1)
            YN_(g)
            if nx:
                M_(g + 1)
            TR_(g)
            pump(nxt_parts, 1)
        pump(nxt_parts, 99)

    I("dve", lambda e: e.memset(S32.ap(), 0.0), [], [S32])
    I("dve", lambda e: e.memset(S_bf.ap(), 0.0), [], [S_bf])
    I("dve", lambda e: e.memset(totsum.ap(), 0.0), [], [totsum])
    I("dve", lambda e: e.memset(vhalo.ap(), 0.0), [], [vhalo])
    XB_PIECES = [(O1 + 512 * q, 4 * q) for q in range(5)]
    ALL_PIECES = [(O1 + 512 * q, 4 * q) for q in range(6)]
    nblk_pre = (NPRE + 3) // 4

    def pre_tiles(blk):
        return [1 + blk * 4 + q for q in range(4) if 1 + blk * 4 + q <= NPRE]

    blocks2 = [[NPRE + 1]] + [[NPRE + 2 + 4 * b + q for q in range(4)] for b in range(4)]
    st["pt2"] = True
    stage_u(pre_tiles(0), HALO + 4 * 128)
    slot0 = wpiece([win[:, O2:O2 + 32]])
    pump(dt_chain(4, False, slot0, mkcol=PP_MK + 0, blocklevel=True, smset=smsets[0]), 99)
    for blk in range(nblk_pre):
        tiles = pre_tiles(blk)
        nt = len(tiles)
        st["W"] = HALO + nt * 128
        st["sm"] = smsets[blk % 2]
        inproj_xbc(nt, XB_PIECES, 0, 0, hcol=PP_MH + blk)
        last = blk + 1 >= nblk_pre
        nxt = blocks2[0] if last else pre_tiles(blk + 1)
        sut = stage_u_thunks(nxt, HALO + len(nxt) * 128)
        slotn = wpiece([win[:, O2:O2 + 32]])
        if last:
            chn = dt_chain(len(nxt), True, slotn, smset=smsets[(blk + 1) % 2])
        else:
            chn = dt_chain(len(nxt), False, slotn, mkcol=PP_MK + blk + 1, blocklevel=True, smset=smsets[(blk + 1) % 2])
        for i in range(nt):
            xB_transposes(i, mkcol=PP_MK + blk)
            state_accum(i, nt)
            pump(sut, 2)
            if not sut:
                pump(chn, 6)
        pump(sut, 99)
        pump(chn, 99)
        state_finish()
        mark("p1b%d" % blk)
    st["pt2"] = False
    I("act", lambda e: e.copy(out=S_bf.ap(), in_=S32.ap()), [S32], [S_bf])
    mark("exch")
    wsso, wpg, wpo, wo, wfi, wfo = (dr["w_ssd_out"], dr["w_pool_grp"], dr["w_pool_out"], dr["w_out"],
                                    dr["w_ffn_in"], dr["w_ffn_out"])
    for bi, tiles in enumerate(blocks2):
        nt = len(tiles)
        TB = nt * 128
        W_ = HALO + TB
        st["W"] = W_
        if bi > 0:
            stage_u(tiles, W_)
        if bi == 0:
            st["sm"] = smsets[nblk_pre % 2]
        else:
            st["sm"] = smsets[0]
            slot = wpiece([win[:, O2:O2 + 32]])
            chain = dt_chain(nt, True, slot)
            pump(chain, 99)
        if bi == 0:
            inproj_xbc(nt, ALL_PIECES, W_, TB)
        else:
            inproj_xbc(nt, ALL_PIECES, HALO if bi == 1 else 0, 0)
        for q in range(4):
            slot = wpiece([win[:, q * 512:(q + 1) * 512]])
            for i in range(nt):
                psd = nextD()
                for k in range(8):
                    mm(psd, psd.ap(), uT.ap(k * W_ + HALO + i * 128, k * W_ + HALO + (i + 1) * 128),
                       slot.ap(k * 512, (k + 1) * 512), k == 0, k == 7, [uT, slot], inc=(k == 7))
                d = siluz.sub(i * 2048 + q * 512, i * 2048 + (q + 1) * 512)
                act(d.ap(), psd.ap(), AF.Silu, [psd], [d])
        xB_transposes(0)
        for i in range(nt):
            ssd_y(i, xB_parts(i + 1, bank=PT3) if i + 1 < nt else None)
            state_update(i)
        mark("ssd%d" % bi)
        for q in range(2):
            slot = wpiece([win[:, O3 + q * 512:O3 + (q + 1) * 512]])
            for j in range(4):
                ch = q * 4 + j
                wide = nextWide()
                formA(wide, slot, 0, 8, 512, j, uT, W_, W_, [uT])
                d = pT.sub(ch * 544, ch * 544 + W_)
                if bi == 1:
                    act(d.ap(0, HALO), wide.ap(0, HALO), AF.Identity, [wide, pp], [d], scale=ppc(PP_HM))
                    act(d.ap(HALO, W_), wide.ap(HALO, W_), AF.Copy, [wide], [d])
                else:
                    act(d.ap(), wide.ap(0, W_), AF.Copy, [wide], [d])
        if True:
            for ch in range(8):
                g = ch // 2
                src = pT.sub(ch * 544, ch * 544 + W_)
                cur = src
                for s_ in range(g + 1):
                    sh = 1 << s_
                    nb = spp[s_ % 2]
                    lo_ = (2 << s_) - 1
                    tt(nb.ap(lo_, W_), cur.ap(lo_, W_), cur.ap(lo_ - sh, W_ - sh), ALU.add, [cur], [nb])
                    cur = nb
                d = pbT.sub(ch * 512, ch * 512 + TB)
                stt(d.ap(), cur.ap(HALO, W_), 1.0 / (1 << (g + 1)), src.ap(HALO, W_), ALU.mult, ALU.subtract,
                    [cur, src], [d])
                if bi == 1:
                    dd = d.sub(0, 32)
                    tt(m1b.ap(0, 32), cur.ap(HALO, HALO + 32), rows.ap(RW_RC + g * 32, RW_RC + (g + 1) * 32), ALU.mult,
                       [cur, rows], [m1b])
                    tt(dd.ap(), m1b.ap(0, 32), src.ap(HALO, HALO + 32), ALU.subtract, [m1b, src], [dd])
        for g in range(4):
            slot = wpiece([wpg[g * 256:(g + 1) * 256, :]])
            for j in range(2):
                ch = g * 2 + j
                psd = nextD()
                for k in range(2):
                    mm(psd, psd.ap(0, TB), slot.ap(k * 256 + j * 128, k * 256 + (j + 1) * 128),
                       pbT.ap((g * 2 + k) * 512, (g * 2 + k) * 512 + TB), k == 0, k == 1, [slot, pbT], inc=(k == 1))
                d = pgT.sub(ch * 512, ch * 512 + TB)
                act(d.ap(), psd.ap(0, TB), AF.Identity, [psd, pp], [d], scale=ppc(PP_PSC + ch))
        for c in range(8):
            slot1 = wpiece([win[:, O4 + c * 128:O4 + (c + 1) * 128], win[:, O4 + 1024 + c * 128:O4 + 1024 + (c + 1) * 128]])
            slot2 = wpiece([wsso[:, c * 128:(c + 1) * 128], wpo[:, c * 128:(c + 1) * 128]])
            for (off, dst) in ((0, gab), (1024, gbb)):
                psd = nextD()
                for k in range(8):
                    mm(psd, psd.ap(0, TB), slot1.ap(off + k * 128, off + (k + 1) * 128),
                       uT.ap(k * W_ + HALO, k * W_ + W_), k == 0, k == 7, [slot1, uT], inc=(k == 7))
                act(dst.ap(0, TB), psd.ap(0, TB), AF.Sigmoid, [psd], [dst])
            psd = nextD()
            for k in range(16):
                mm(psd, psd.ap(0, TB), slot2.ap(k * 128, (k + 1) * 128), ynT.ap(k * 512, k * 512 + TB),
                   k == 0, k == 15, [slot2, ynT], inc=(k == 15))
            tt(m1b.ap(0, TB), psd.ap(0, TB), gab.ap(0, TB), ALU.mult, [psd, gab], [m1b])
            psd = nextD()
            for k in range(8):
                mm(psd, psd.ap(0, TB), slot2.ap(2048 + k * 128, 2048 + (k + 1) * 128), pgT.ap(k * 512, k * 512 + TB),
                   k == 0, k == 7, [slot2, pgT], inc=(k == 7))
            tt(gbb.ap(0, TB), psd.ap(0, TB), gbb.ap(0, TB), ALU.mult, [psd, gbb], [gbb])
            d = mergedT.sub(c * 512, c * 512 + TB)
            tt(d.ap(), m1b.ap(0, TB), gbb.ap(0, TB), ALU.add, [m1b, gbb], [d])
        mark("merge%d" % bi)
        xres = []
        for q in range(2):
            slot = wpiece([wo[:, q * 512:(q + 1) * 512]])
            for i, tl in enumerate(tiles):
                psd = nextD()
                for k in range(8):
                    mm(psd, psd.ap(), mergedT.ap(k * 512 + i * 128, k * 512 + (i + 1) * 128),
                       slot.ap(k * 512, (k + 1) * 512), k == 0, k == 7, [mergedT, slot], inc=(k == 7))
                t_ = tg[i % 2]
                tt(t_.ap(), psd.ap(), g1bc.ap(q * 512, (q + 1) * 512), ALU.mult, [psd, g1bc], [t_])
                hd = hbuf.sub(i * 1024 + q * 512, i * 1024 + (q + 1) * 512)
                xb_ = xt[(i * 2 + q) % 3]
                dma_in(xb_, lambda xb_=xb_: xb_.ap(0, 512), lambda tl=tl, q=q: x_d[tl * 128:(tl + 1) * 128, q * 512:(q + 1) * 512])
                tt(hd.ap(), t_.ap(), xb_.ap(0, 512), ALU.add, [t_, xb_], [hd])
        I("dve", lambda e, W_=W_: e.tensor_copy(out=vT.ap(0, 8 * W_).rearrange("p (k c) -> p k c", k=8)[:, :, 0:HALO],
                                         in_=vhalo.ap().rearrange("p (k c) -> p k c", k=8)), [vhalo], [vT])
        for i in range(nt):
            hb = hbuf.sub(i * 1024, (i + 1) * 1024)
            norm_tile(lambda hb=hb: hb.ap(), hb, 128, vT, HALO + i * 128, A2, SH2, W_)
        I("dve", lambda e, W_=W_, TB=TB: e.tensor_copy(out=vhalo.ap().rearrange("p (k c) -> p k c", k=8),
                                         in_=vT.ap(0, 8 * W_).rearrange("p (k c) -> p k c", k=8)[:, :, TB:TB + HALO]),
          [vT], [vhalo])
        mark("norm2_%d" % bi)
        if bi == 0:
            continue
        pend = None
        fw_i = 0
        for q in range(6):
            nco = min(512, FF - q * 512)
            if pend is not None:
                pend()
                pend = None
            slot = wpiece([wfi[:, FF + q * 512:FF + q * 512 + nco]])
            slotv = wpiece([wfi[:, q * 512:q * 512 + nco]])
            for j in range(nco // 128):
                ch = q * 4 + j
                wide = WIDE[1 + (fw_i % 2)]
                fw_i += 1
                formA(wide, slot, 0, 8, nco, j, vT, W_, W_, [vT])
                if bi == 1:
                    act(wide.ap(0, HALO), wide.ap(0, HALO), AF.Identity, [wide, pp], [wide.sub(0, HALO)], scale=ppc(PP_HM))
                ga_ = gacc[ch % 2]
                act(ga_.ap(0, TB), wide.ap(HALO, W_), AF.Identity, [wide, pp], [ga_],
                    scale=ppc(PP_FW + ch * 3 + 2), bias=ppc(PP_FB + ch))
                for kk in (1, 2):
                    stt(ga_.ap(0, TB), wide.ap(HALO - kk, W_ - kk), ppc(PP_FW + ch * 3 + 2 - kk), ga_.ap(0, TB),
                        ALU.mult, ALU.add, [wide, pp, ga_], [ga_])

                def part2(ch=ch, ga_=ga_, slotv=slotv, nco=nco, j=j):
                    act(ga_.ap(0, TB), ga_.ap(0, TB), AF.Silu, [ga_], [ga_])
                    psd = nextD()
                    for k in range(8):
                        mm(psd, psd.ap(0, TB), slotv.ap(k * nco + j * 128, k * nco + (j + 1) * 128),
                           vT.ap(k * W_ + HALO, k * W_ + W_), k == 0, k == 7, [slotv, vT], inc=(k == 7))
                    d = hiddenT.sub(ch * 512, ch * 512 + TB)
                    tt(d.ap(), psd.ap(0, TB), ga_.ap(0, TB), ALU.mult, [psd, ga_], [d])
                if pend is not None:
                    pend()
                pend = part2
        pend()
        mark("ffnin%d" % bi)
        for q in range(4):
            slA = wpiece([wfo[0:1408, q * 256:(q + 1) * 256]])
            slB = wpiece([wfo[1408:2816, q * 256:(q + 1) * 256]])
            pst = [D0.sub(0, 256), D1.sub(0, 256), Qb.sub(0, 256), YO.sub(0, 256)]
            for (sl, kbase) in ((slA, 0), (slB, 11)):
                for i in range(nt):
                    for k in range(11):
                        kk = kbase + k
                        mm(pst[i], pst[i].ap(), hiddenT.ap(kk * 512 + i * 128, kk * 512 + (i + 1) * 128),
                           sl.ap(k * 256, (k + 1) * 256), kk == 0, kk == 21, [hiddenT, sl], inc=(k == 10))
            for i in range(nt):
                t_ = tg[i % 2]
                tt(t_.ap(0, 256), pst[i].ap(), g2bc.ap(q * 256, (q + 1) * 256), ALU.mult, [pst[i], g2bc], [t_])
                hd = hbuf.sub(i * 1024 + q * 256, i * 1024 + (q + 1) * 256)
                tt(hd.ap(), hd.ap(), t_.ap(0, 256), ALU.add, [hd, t_], [hd])
        for i, tl in enumerate(tiles):
            hb = hbuf.sub(i * 1024, (i + 1) * 1024)
            ob = outb[i % 2]
            act(ob.ap(), hb.ap(), AF.Square, [hb], [ob, ssq], accum_out=ssq.ap(0, 1))
            act(ssq.ap(1, 2), ssq.ap(0, 1), AF.Sqrt, [ssq], [ssq], scale=1.0 / D, bias=EPS)
            I("dve", lambda e: e.reciprocal(out=ssq.ap(2, 3), in_=ssq.ap(1, 2)), [ssq], [ssq])
            stt(ob.ap(), hb.ap(), ssq.ap(2, 3), rows.ap(RW_GFIN, RW_GFIN + 1024), ALU.mult, ALU.mult,
                [hb, ssq, rows], [ob])
            r0 = (tl - NPRE - 2) * 128
            kb.Dm("sp", lambda r0=r0: y_d[r0:r0 + 128, :], lambda ob=ob: ob.ap(), R=[ob], is_out=True)
        mark("blk%d" % bi)


def _host_tables(inp, core):
    f = np.float32
    pp = np.zeros((128, NPP), f)

    def colmajor(v):
        v = np.asarray(v, f).reshape(-1, 128)
        return v.T

    pp[:, PP_C:PP_C + 8] = colmajor(inp["c"][0])
    b = inp["b_ada"][0]
    pp[:, PP_G1:PP_G1 + 8] = colmajor(inp["g_norm1"][0])
    pp[:, PP_BSH1:PP_BSH1 + 8] = colmajor(b[0:1024])
    pp[:, PP_BSC1:PP_BSC1 + 8] = colmajor(b[1024:2048])
    pp[:, PP_G2:PP_G2 + 8] = colmajor(inp["g_norm2"][0])
    pp[:, PP_BSH2:PP_BSH2 + 8] = colmajor(b[3072:4096])
    pp[:, PP_BSC2:PP_BSC2 + 8] = colmajor(b[4096:5120])
    cw = inp["ssd_conv_w"][0]
    for k in range(4):
        pp[:, PP_CW + k:PP_CW + 96:4] = colmajor(cw[k])
    pp[:, PP_CB:PP_CB + 24] = colmajor(inp["ssd_conv_b"][0])
    fw = inp["ffn_conv_w"][0]
    for k in range(3):
        pp[:, PP_FW + k:PP_FW + 66:3] = colmajor(fw[k])
    pp[:, PP_FB:PP_FB + 22] = colmajor(inp["ffn_conv_b"][0])
    pp[:, PP_GSSD:PP_GSSD + 16] = colmajor(inp["g_ssd_norm"][0])
    pp[:, PP_PSC:PP_PSC + 8] = colmajor(inp["pool_scale"][0])
    pp[:, PP_HM] = 0.0 if core == 0 else 1.0
    for j in range(8):
        pp[:, PP_SEL + j] = 1.0 if j < core else 0.0
    for bb in range(28):
        pp[:, PP_MK + bb] = 1.0 if bb >= 28 - 4 * core else 0.0
        pp[:, PP_MH + bb] = 1.0 if (bb - 1) >= 28 - 4 * core else 0.0
    rows = np.zeros((128, NRW), f)
    rows[:, RW_GFIN:RW_GFIN + 1024] = inp["g_final"][None, :]
    rows[:, RW_DTB:RW_DTB + 32] = inp["ssd_dt_bias"][0][None, :]
    rows[:, RW_ALOG:RW_ALOG + 32] = inp["ssd_a_log"][0][None, :]
    rows[:, RW_DSK:RW_DSK + 32] = inp["ssd_d"][0][None, :]
    for g in range(4):
        w = 1 << (g + 1)
        t = np.arange(32) + core * T
        cnt = np.minimum(t + 1, w).astype(f)
        rows[:, RW_RC + g * 32:RW_RC + (g + 1) * 32] = (1.0 / cnt)[None, :]
    bg = np.zeros((128, 2048), f)
    bg[:, 0:1024] = b[2048:3072][None, :]
    bg[:, 1024:2048] = b[5120:6144][None, :]
    return pp, rows, bg


def kernel(**inputs):
    inp = {k: np.asarray(v) for k, v in inputs.items()}
    if "nc" not in _NC_CACHE:
        _NC_CACHE["nc"] = build_program()
    nc = _NC_CACHE["nc"]
    f = np.float32
    x = np.ascontiguousarray(inp["x"][0], dtype=f)
    PADT = 14336 + 128
    xpad = np.concatenate([np.zeros((PADT, D), f), x], axis=0)
    cst = np.zeros((128, 384), f)
    cst[:, 0:128] = np.eye(128, dtype=f)
    cst[:, 128:256] = np.triu(np.ones((128, 128), f))
    cst[:, 256:384] = 1.0
    shared = {
        "w_ada": np.ascontiguousarray(inp["w_ada"][0], f),
        "w_in": np.ascontiguousarray(inp["w_in"][0], f),
        "w_ssd_out": np.ascontiguousarray(inp["w_ssd_out"][0], f),
        "w_pool_grp": np.ascontiguousarray(inp["w_pool_grp"][0].reshape(1024, 256), f),
        "w_pool_out": np.ascontiguousarray(inp["w_pool_out"][0], f),
        "w_out": np.ascontiguousarray(inp["w_out"][0], f),
        "w_ffn_in": np.ascontiguousarray(inp["w_ffn_in"][0], f),
        "w_ffn_out": np.ascontiguousarray(inp["w_ffn_out"][0], f),
        "cst": cst,
    }
    in_maps = []
    for c in range(NCORES):
        pp, rows, bg = _host_tables(inp, c)
        m = dict(shared)
        m["x"] = np.ascontiguousarray(xpad[c * T:c * T + NTS * 128])
        m["pp"] = pp
        m["rows"] = rows
        m["bgate"] = bg
        in_maps.append(m)
    res = run_bass_kernel_spmd(nc, in_maps, core_ids=list(range(NCORES)))
    out = np.concatenate([np.asarray(r["y"]) for r in res.results], axis=0)
    return out.reshape(1, NCORES * T, D).astype(f)
```

```python
import numpy as np
from contextlib import ExitStack
import concourse.bass as bass
import concourse.mybir as mybir
from concourse.bass_utils import run_bass_kernel_spmd

F32 = mybir.dt.float32
BF16 = mybir.dt.bfloat16
AF = mybir.ActivationFunctionType
ALU = mybir.AluOpType

NCORES = 8
D = 1024
KC = 8
T = 2048
NPRE = 111
NTS = 18 + NPRE
HALO = 32
NH = 32
HD = 64
O1 = 2048
O2 = O1 + 3072
O3 = O2 + 32
O4 = O3 + 1024
FF = 2816
FFC = 22
EPS = 1e-6
SEM_CAP = 12000
_NC_CACHE = {}

PP_C = 0
PP_G1 = 8
PP_BSH1 = 16
PP_BSC1 = 24
PP_G2 = 32
PP_BSH2 = 40
PP_BSC2 = 48
PP_CW = 56
PP_CB = PP_CW + 96
PP_FW = PP_CB + 24
PP_FB = PP_FW + 66
PP_GSSD = PP_FB + 22
PP_PSC = PP_GSSD + 16
PP_HM = PP_PSC + 8
PP_SEL = PP_HM + 1
PP_MK = PP_SEL + 8
PP_MH = PP_MK + 28
NPP = PP_MH + 28
RW_GFIN = 0
RW_DTB = 1024
RW_ALOG = 1056
RW_DSK = 1088
RW_RC = 1120
NRW = RW_RC + 128


class _Stop(Exception):
    pass


KSTOP = None


class Arena:
    def __init__(self, t, ncols):
        self.t = t
        self.ncols = ncols
        self.recs = []
        self.pos = 0
        self.bank = None


class Buf:
    def __init__(self, arena, lo, n):
        self.arena = arena
        self.lo = lo
        self.n = n
        self.hi = lo + n

    def ap(self, a=0, b=None, p=128):
        b = self.n if b is None else b
        return self.arena.t[0:p, self.lo + a:self.lo + b]

    def sub(self, a, b):
        return Buf(self.arena, self.lo + a, b - a)


class BufV(Buf):
    def ap(self, a=0, b=None, p=128):
        b = 2 * self.n if b is None else b
        return self.arena.t[0:p, self.lo:self.hi].bitcast(BF16)[:, a:b]


class KB:
    ENGS = ("pe", "act", "dve", "pool", "sp")

    def __init__(self, nc, es, dry):
        self.nc = nc
        self.dry = dry
        self.q = {e: [] for e in self.ENGS}
        self.cnt = {e: 0 for e in self.ENGS}
        self.semi = {e: 0 for e in self.ENGS}
        self.sems = {}
        self.known = {e: {} for e in self.ENGS}
        self.ndsem = 10
        self.dsems = {}
        self.dcnt = {}
        self.dnext = {"sp": 0, "pool": 0, "act": 0}
        self.out_toks = []
        self.log = {e: [] for e in self.ENGS}
        if not dry:
            for e in self.ENGS:
                self.sems[e] = [es.enter_context(nc.semaphore("s_%s_%d" % (e, i))) for i in range(4)]
            for qn in ("sp", "pool"):
                self.dsems[qn] = [es.enter_context(nc.semaphore("d_%s_%d" % (qn, i))) for i in range(self.ndsem)]
                self.dcnt[qn] = [0] * self.ndsem

    def _wait(self, eng, tok):
        key, val, src = tok
        if src == eng and eng == "pe":
            return
        if self.known[eng].get(key, 0) >= val:
            return
        self.known[eng][key] = val
        sem = self._semobj(key)
        self.log[eng].append(('w', key, val))
        self.q[eng].append(lambda e, s=sem, v=val: e.wait_ge(s, v))

    def _semobj(self, key):
        kind, name, i = key
        if kind == "e":
            return self.sems[name][i]
        return self.dsems[name][i]

    @staticmethod
    def _rng(b):
        bk = b.arena.bank
        if bk is None:
            return b.lo, b.hi
        return (b.lo // bk) * bk, ((b.hi + bk - 1) // bk) * bk

    def _deps(self, eng, R, W):
        toks = []
        for (lst, isw) in ((R, False), (W, True)):
            for b in lst:
                lo, hi = self._rng(b)
                ps = b.arena.bank is not None
                for r in b.arena.recs:
                    if r[0] < hi and lo < r[1]:
                        if isw or r[2] or (ps and r[3][2] != eng):
                            toks.append(r[3])
        for t in toks:
            self._wait(eng, t)

    def _record(self, eng, R, W, tok):
        for (lst, isw) in ((R, False), (W, True)):
            for b in lst:
                a = b.arena
                lo, hi = self._rng(b)
                if isw or a.bank is not None:
                    a.recs = [r for r in a.recs if not (lo <= r[0] and r[1] <= hi)]
                else:
                    a.recs = [r for r in a.recs if not ((not r[2]) and r[3][2] == eng and r[3][0][0] == "e"
                                                        and tok[0][0] == "e" and lo <= r[0] and r[1] <= hi)]
                a.recs.append((lo, hi, isw, tok))

    def I(self, eng, fn, R=(), W=(), inc=True):
        if self.dry:
            return
        self._deps(eng, R, W)
        if inc:
            if self.cnt[eng] >= SEM_CAP:
                self.semi[eng] += 1
                self.cnt[eng] = 0
            self.cnt[eng] += 1
            val = self.cnt[eng]
            si = self.semi[eng]
            sem = self.sems[eng][si]
            self.log[eng].append(('i', ('e', eng, si), 1))
            self.q[eng].append(lambda e, fn=fn, sem=sem: fn(e).then_inc(sem, 1))
        else:
            val = self.cnt[eng] + 1
            si = self.semi[eng]
            assert val <= SEM_CAP
            self.log[eng].append(('n', None, 0))
            self.q[eng].append(lambda e, fn=fn: fn(e))
        tok = (("e", eng, si), val, eng)
        self._record(eng, R, W, tok)

    def Dm(self, qn, out_fn, in_fn, R=(), W=(), is_out=False, **kw):
        if self.dry:
            return
        self._deps(qn, R, W)
        j = self.dnext[qn]
        self.dnext[qn] = (j + 1) % self.ndsem
        key = ("d", qn, j)
        if self.dcnt[qn][j] > 0:
            self._wait(qn, (key, 16 * self.dcnt[qn][j], "dma"))
        self.dcnt[qn][j] += 1
        val = 16 * self.dcnt[qn][j]
        sem = self.dsems[qn][j]
        self.log[qn].append(('i', key, 16))
        self.q[qn].append(lambda e, o=out_fn, i=in_fn, sem=sem, kw=kw: e.dma_start(out=o(), in_=i(), **kw).then_inc(sem, 16))
        tok = (key, val, "dma")
        self._record("dma", R, W, tok)
        if is_out:
            self.out_toks.append(tok)

    def finish(self):
        if self.dry:
            return
        for t in self.out_toks:
            self._wait("sp", t)


def build_program():
    nc = bass.Bass("TRN2", target_bir_lowering=False)
    dr = {}

    def din(name, shape):
        dr[name] = nc.dram_tensor(name, shape, F32, kind="ExternalInput").ap()
        return dr[name]

    x_d = din("x", [NTS * 128, D])
    wada_d = din("w_ada", [D, 6 * D])
    win_d = din("w_in", [D, 8224])
    wsso_d = din("w_ssd_out", [2048, D])
    wpg_d = din("w_pool_grp", [1024, 256])
    wpo_d = din("w_pool_out", [D, D])
    wo_d = din("w_out", [D, D])
    wfi_d = din("w_ffn_in", [D, 2 * FF])
    wfo_d = din("w_ffn_out", [FF, D])
    pp_d = din("pp", [128, NPP])
    rw_d = din("rows", [128, NRW])
    bg_d = din("bgate", [128, 2048])
    cst_d = din("cst", [128, 384])
    y_d = nc.dram_tensor("y", [T, D], F32, kind="ExternalOutput").ap()
    cc_in = nc.dram_tensor("cc_in", [128, 2080], F32, kind="Internal").ap()
    cc_out = nc.dram_tensor("cc_out", [NCORES * 128, 2080], F32, kind="Internal").ap()

    A16N = 3472 + 12544 + 36608
    A32N = 12160 + 8256
    with ExitStack() as es:
        w16_t = es.enter_context(nc.sbuf_tensor("w16", [128, 3 * 4096], BF16))
        a16_t = es.enter_context(nc.sbuf_tensor("a16", [128, A16N], BF16))
        a32_t = es.enter_context(nc.sbuf_tensor("a32", [128, A32N], F32))
        ps_t = es.enter_context(nc.psum_tensor("ps", [128, 3584], F32))
        pt_t = es.enter_context(nc.psum_tensor("pt", [128, 1024], BF16))
        kb_real = KB(nc, es, False)
        plan = []
        for dry in (True, False):
            kb = KB(nc, es, True) if dry else kb_real
            arenas = dict(w16=Arena(w16_t, 3 * 4096), a16=Arena(a16_t, A16N), a32=Arena(a32_t, A32N),
                          ps=Arena(ps_t, 3584), pt=Arena(pt_t, 1024), cc=Arena(None, 4))
            try:
                arenas['ps'].bank = 512
                arenas['pt'].bank = 1024
                emit(nc, kb, arenas, dr, y_d, cc_in, cc_out, plan, dry)
            except _Stop:
                pass
        kb = kb_real
        kb.finish()
        _NC_CACHE['kb'] = kb
        block = es.enter_context(nc.Block())

        @block.sync
        def _(e):
            for f in kb.q["sp"]:
                f(e)

        @block.gpsimd
        def _(e):
            for f in kb.q["pool"]:
                f(e)

        @block.tensor
        def _(e):
            for f in kb.q["pe"]:
                f(e)

        @block.vector
        def _(e):
            for f in kb.q["dve"]:
                f(e)

        @block.scalar
        def _(e):
            for f in kb.q["act"]:
                f(e)
    return nc


def emit(nc, kb, AR, dr, y_d, cc_in, cc_out, plan, dry):
    I = kb.I
    x_d = dr["x"]

    def mark(name):
        if KSTOP is not None and name == KSTOP:
            raise _Stop()

    def alloc(an, n):
        a = AR[an]
        b = Buf(a, a.pos, n)
        a.pos += n
        assert a.pos <= a.ncols, (an, a.pos, a.ncols)
        return b

    wslots = [alloc("w16", 4096) for _ in range(3)]
    S_bf = alloc("a16", 2048)
    ident_bf = alloc("a16", 128)
    vhalo = alloc("a16", 256)
    cond_bf = alloc("a16", 16)
    cond_bc = alloc("a16", 1024)
    AR["a16"].pos = 3472
    uT = alloc("a16", 8 * 544)
    ynT = alloc("a16", 16 * 512)
    X16 = AR["a16"].pos
    xBCc = alloc("a16", 24 * 512)
    siluz = alloc("a16", 4 * 2048)
    XS = [dict(x_tm=alloc("a16", 2048), xw=alloc("a16", 2048), xD=alloc("a16", 2048), B_tm=alloc("a16", 512))
          for _ in range(2)]
    Mbs = [alloc("a16", 1024) for _ in range(2)]
    ynb = alloc("a16", 512)
    AR["a16"].pos = X16
    pbT = alloc("a16", 8 * 512)
    pgT = alloc("a16", 8 * 512)
    mergedT = alloc("a16", 8 * 512)
    hn = alloc("a16", 1024)
    hn2 = alloc("a16", 1024)
    hnb = [hn, hn2]
    vT = alloc("a16", 8 * 544)
    hiddenT = alloc("a16", 22 * 512)

    S32 = alloc("a32", 2048)
    g1bc = alloc("a32", 1024)
    g2bc = alloc("a32", 1024)
    rows = alloc("a32", NRW)
    cst = alloc("a32", 384)
    pp = alloc("a32", NPP)
    modp = alloc("a32", 32)
    a_bc = alloc("a32", 32)
    totsum = alloc("a32", 32)
    Dj = alloc("a32", 32)
    smA = [alloc("a32", 128) for _ in range(11)]
    smB = [alloc("a32", 128) for _ in range(11)]
    smsets = [smA, smB]
    sm = smA
    ssq = alloc("a32", 16)
    xt = [alloc("a32", 1024) for _ in range(3)]
    assert AR["a32"].pos <= 12160, AR["a32"].pos
    AR["a32"].pos = 12160
    Y32 = 12160
    CBm = alloc("a32", 512)
    Rbs = [alloc("a32", 1024) for _ in range(2)]
    Ebs = [alloc("a32", 1024) for _ in range(2)]
    sqj = alloc("a32", 512)
    tb = [alloc("a32", 512) for _ in range(2)]
    cacc = [alloc("a32", 544) for _ in range(2)]
    AR["a32"].pos = Y32
    Gx = [alloc("a32", 2080) for _ in range(2)]
    AR["a32"].pos = Y32
    bgt = alloc("a32", 2048)
    AR["a32"].pos = Y32
    pT = alloc("a32", 8 * 544)
    spp = [alloc("a32", 544) for _ in range(2)]
    gab = alloc("a32", 512)
    gbb = alloc("a32", 512)
    m1b = alloc("a32", 512)
    AR["a32"].pos = Y32
    hbuf = alloc("a32", 4096)
    tg = [alloc("a32", 512) for _ in range(2)]
    gacc = [alloc("a32", 544) for _ in range(2)]
    outb = [alloc("a32", 1024) for _ in range(2)]

    PS = AR["ps"]
    PT = Buf(AR["pt"], 0, 1024)
    D0 = Buf(PS, 0, 512)
    D1 = Buf(PS, 512, 512)
    Qb = Buf(PS, 1024, 512)
    ABb = Buf(PS, 1536, 1024)
    YD = Buf(PS, 2560, 512)
    YO = Buf(PS, 3072, 512)
    WIDE = [Buf(PS, 0, 1024), Buf(PS, 1024, 1024), Buf(PS, 2048, 1024)]
    PT2 = BufV(PS, 2560, 512)
    PT4 = BufV(PS, 2048, 512)
    PTS = [PT, PT2, PT4]
    st = dict(d=0, wide=0, wi=0, xt=0, pt=0, pt2=False, sm=None)

    def nextPT():
        if not st["pt2"]:
            return PT
        st["pt"] = (st["pt"] + 1) % 3
        return PTS[st["pt"]]

    def nextD():
        st["d"] ^= 1
        return (D0, D1)[st["d"]]

    def nextWide():
        st["wide"] = (st["wide"] + 1) % 3
        return WIDE[st["wide"]]

    ident32 = cst.sub(0, 128)
    tri32 = cst.sub(128, 256)
    ones32 = cst.sub(256, 384)

    def ppc(col, p=128):
        return pp.ap(col, col + 1, p)

    def wpiece(parts):
        idx = st["wi"]
        st["wi"] += 1
        if dry:
            plan.append(parts)
            return wslots[idx % 3]

        def issue(k):
            if k >= len(plan) or k < st.get("issued", 0):
                return
            st["issued"] = k + 1
            slot = wslots[k % 3]
            off = 0
            for src in plan[k]:
                rows_, nco = src.shape
                kc = rows_ // 128
                dst = slot.sub(off, off + kc * nco)
                kb.Dm("pool",
                      (lambda dst=dst, kc=kc: dst.ap().rearrange("p (k c) -> p k c", k=kc)),
                      (lambda src=src: src.rearrange("(k p) c -> p k c", p=128)),
                      W=[dst])
                off += kc * nco
        for k in range(st.get("issued", 0), idx + 2):
            issue(k)
        return wslots[idx % 3]

    def act(out, in_, func, R, W, **kw):
        I("act", lambda e: e.activation(out=out, in_=in_, func=func, **kw), R, W)

    def tt(out, in0, in1, op, R, W, eng="dve"):
        I(eng, lambda e: e.tensor_tensor(out=out, in0=in0, in1=in1, op=op), R, W)

    def ts(out, in0, s1, op0, R, W, s2=None, op1=None, eng="dve"):
        if op1 is None:
            I(eng, lambda e: e.tensor_scalar(out=out, in0=in0, scalar1=s1, scalar2=None, op0=op0), R, W)
        else:
            I(eng, lambda e: e.tensor_scalar(out=out, in0=in0, scalar1=s1, scalar2=s2, op0=op0, op1=op1), R, W)

    def stt(out, in0, sc, in1, op0, op1, R, W):
        I("dve", lambda e: e.scalar_tensor_tensor(out=out, in0=in0, scalar=sc, in1=in1, op0=op0, op1=op1), R, W)

    def mm(outb_, out, lhsT, rhs, start, stop, R, inc):
        I("pe", lambda e: e.matmul(out, lhsT=lhsT, rhs=rhs, start=start, stop=stop), R, [outb_], inc=inc)

    def tr(outb_, out, in_, idn, R, inc=True):
        I("pe", lambda e: e.transpose(out, in_, idn), R, [outb_], inc=inc)

    def dma_in(dstb, dst_fn, src_fn):
        kb.Dm("sp", dst_fn, src_fn, W=[dstb])

    dma_in(cst, lambda: cst.ap(), lambda: dr["cst"][:, :])
    dma_in(pp, lambda: pp.ap(), lambda: dr["pp"][:, :])
    dma_in(rows, lambda: rows.ap(), lambda: dr["rows"][:, :])
    dma_in(bgt, lambda: bgt.ap(), lambda: dr["bgate"][:, :])
    I("dve", lambda e: e.tensor_copy(out=ident_bf.ap(), in_=ident32.ap()), [ident32], [ident_bf])
    act(sm[0].ap(0, 8), pp.ap(PP_C, PP_C + 8), AF.Silu, [pp], [sm[0]])
    I("dve", lambda e: e.tensor_copy(out=cond_bf.ap(0, 8), in_=sm[0].ap(0, 8)), [sm[0]], [cond_bf])
    for k in range(8):
        I("dve", lambda e, k=k: e.tensor_copy(out=cond_bc.ap(k * 128, (k + 1) * 128),
                                              in_=sm[0].ap(k, k + 1).broadcast_to([128, 128])),
          [sm[0]], [cond_bc.sub(k * 128, (k + 1) * 128)])
    act(a_bc.ap(), rows.ap(RW_ALOG, RW_ALOG + 32), AF.Exp, [rows], [a_bc])
    ts(a_bc.ap(), a_bc.ap(), -1.0, ALU.mult, [a_bc], [a_bc])
    wada = dr["w_ada"]
    for (c0, dstcol, bcol, gcol, is_scale) in ((0, 8, PP_BSH1, None, False), (1024, 0, PP_BSC1, PP_G1, True),
                                               (3072, 24, PP_BSH2, None, False), (4096, 16, PP_BSC2, PP_G2, True)):
        psd = nextD()
        for half in range(2):
            slot = wpiece([wada[:, c0 + half * 512:c0 + (half + 1) * 512]])
            for jj in range(4):
                j = half * 4 + jj
                for k in range(8):
                    mm(psd, psd.ap(j, j + 1), slot.ap(k * 512 + jj * 128, k * 512 + (jj + 1) * 128),
                       cond_bf.ap(k, k + 1), k == 0, k == 7, [slot, cond_bf], inc=(k == 7))
        d = modp.sub(dstcol, dstcol + 8)
        tt(d.ap(), psd.ap(0, 8), pp.ap(bcol, bcol + 8), ALU.add, [psd, pp], [d])
        if is_scale:
            ts(d.ap(), d.ap(), 1.0, ALU.add, [d], [d])
            tt(d.ap(), d.ap(), pp.ap(gcol, gcol + 8), ALU.mult, [d, pp], [d])
    for (c0, gb_, boff) in ((2048, g1bc, 0), (5120, g2bc, 1024)):
        for half in range(2):
            slot = wpiece([wada[:, c0 + half * 512:c0 + (half + 1) * 512]])
            psd = nextD()
            for k in range(8):
                mm(psd, psd.ap(), cond_bc.ap(k * 128, (k + 1) * 128), slot.ap(k * 512, (k + 1) * 512),
                   k == 0, k == 7, [slot, cond_bc], inc=(k == 7))
            d = gb_.sub(half * 512, (half + 1) * 512)
            tt(d.ap(), psd.ap(), bgt.ap(boff + half * 512, boff + (half + 1) * 512), ALU.add, [psd, bgt], [d])

    A1, SH1, A2, SH2 = 0, 8, 16, 24
    st["sm"] = smA
    mark("prologue")

    def pump(ch, n=1):
        for _ in range(n):
            if ch:
                ch.pop(0)()

    def norm_A(src_ap_fn, srcb, np_, slot_):
        hb_ = hnb[slot_]
        sq_ = ssq.sub(8 + slot_ * 4, 12 + slot_ * 4)
        I("dve", lambda e: e.scalar_tensor_tensor(out=hb_.ap(0, 1024, np_), in0=src_ap_fn(), scalar=1.0, in1=src_ap_fn(),
                                                  op0=ALU.mult, op1=ALU.mult, accum_out=sq_.ap(0, 1, np_)),
          [srcb], [hb_, sq_])
        act(sq_.ap(1, 2, np_), sq_.ap(0, 1, np_), AF.Sqrt, [sq_], [sq_], scale=1.0 / D, bias=EPS)
        I("dve", lambda e: e.reciprocal(out=sq_.ap(2, 3, np_), in_=sq_.ap(1, 2, np_)), [sq_], [sq_])
        ts(hb_.ap(0, 1024, np_), src_ap_fn(), sq_.ap(2, 3, np_), ALU.mult, [srcb, sq_], [hb_])

    def norm_B(np_, slot_, dstT, dcol, acol, shcol, width):
        hb_ = hnb[slot_]
        for k in range(8):
            P_ = PT if k < 4 else PT2
            kk = k % 4
            tr(P_, P_.ap(kk * 128, kk * 128 + np_), hb_.ap(k * 128, (k + 1) * 128, np_), ident_bf.ap(0, np_, np_),
               [hb_, ident_bf], inc=(kk == 3))
        for k in range(4):
            d = dstT.sub(k * width + dcol, k * width + dcol + np_)
            act(d.ap(), PT.ap(k * 128, k * 128 + np_), AF.Identity, [PT, modp], [d],
                scale=modp.ap(acol + k, acol + k + 1), bias=modp.ap(shcol + k, shcol + k + 1))
            k2 = k + 4
            d2 = dstT.sub(k2 * width + dcol, k2 * width + dcol + np_)
            ts(d2.ap(), PT2.ap(k * 128, k * 128 + np_), modp.ap(acol + k2, acol + k2 + 1), ALU.mult, [PT2, modp], [d2],
               s2=modp.ap(shcol + k2, shcol + k2 + 1), op1=ALU.add)

    def norm_tile(src_ap_fn, srcb, np_, dstT, dcol, acol, shcol, width):
        norm_A(src_ap_fn, srcb, np_, 0)
        norm_B(np_, 0, dstT, dcol, acol, shcol, width)

    def load_x(row0, np_):
        b = xt[st["xt"]]
        st["xt"] = (st["xt"] + 1) % 3
        dma_in(b, lambda: b.ap(0, 1024, np_), lambda: x_d[row0:row0 + np_, :])
        return b

    def stage_u_thunks(tiles, width):
        t0 = tiles[0]
        items = [(t0 * 128 - HALO, HALO, 0)] + [(tl * 128, 128, HALO + i * 128) for i, tl in enumerate(tiles)]
        bufs_ = []
        th = []

        def first():
            bufs_.append(load_x(items[0][0], items[0][1]))
            if len(items) > 1:
                bufs_.append(load_x(items[1][0], items[1][1]))
            norm_A((lambda b=bufs_[0], n=items[0][1]: b.ap(0, 1024, n)), bufs_[0], items[0][1], 0)
        th.append(first)
        for j, (row0, np_, dcol) in enumerate(items):
            def step(j=j, np_=np_, dcol=dcol):
                if j + 2 < len(items):
                    bufs_.append(load_x(items[j + 2][0], items[j + 2][1]))
                if j + 1 < len(items):
                    nb_ = bufs_[j + 1]
                    norm_A((lambda b=nb_, n=items[j + 1][1]: b.ap(0, 1024, n)), nb_, items[j + 1][1], (j + 1) % 2)
                norm_B(np_, j % 2, uT, dcol, A1, SH1, width)
            th.append(step)
        return th

    def stage_u(tiles, width):
        pump(stage_u_thunks(tiles, width), 99)

    def formA(wide, slot, soff, kc, ncol_w, j, rhsT, width, ncols, rbufs):
        segs = [(0, min(512, ncols))]
        if ncols > 512:
            segs.append((512, ncols))
        for (a, b) in segs:
            for k in range(kc):
                mm(wide.sub(a, b), wide.ap(a, b),
                   slot.ap(soff + k * ncol_w + j * 128, soff + k * ncol_w + (j + 1) * 128),
                   rhsT.ap(k * width + a, k * width + b), k == 0, k == kc - 1, [slot] + rbufs, inc=(k == kc - 1))

    def conv_p1(wide, ch, TB, premask, hcol=None):
        W_ = HALO + TB
        if hcol is not None:
            act(wide.ap(0, HALO), wide.ap(0, HALO), AF.Identity, [wide, pp], [wide.sub(0, HALO)], scale=ppc(hcol))
        if premask > 0:
            ts(wide.ap(0, premask), wide.ap(0, premask), ppc(PP_HM), ALU.mult, [wide, pp], [wide.sub(0, premask)])
        ca = cacc[ch % 2]
        act(ca.ap(0, TB), wide.ap(HALO, W_), AF.Identity, [wide, pp], [ca],
            scale=ppc(PP_CW + ch * 4 + 3), bias=ppc(PP_CB + ch))
        for kk in (1, 2, 3):
            stt(ca.ap(0, TB), wide.ap(HALO - kk, W_ - kk), ppc(PP_CW + ch * 4 + 3 - kk), ca.ap(0, TB),
                ALU.mult, ALU.add, [wide, pp, ca], [ca])

    def conv_p2(ch, TB, postmask):
        ca = cacc[ch % 2]
        d = xBCc.sub(ch * 512, ch * 512 + TB)
        act(d.ap(), ca.ap(0, TB), AF.Silu, [ca], [d])
        if postmask > 0:
            dd = xBCc.sub(ch * 512, ch * 512 + postmask)
            ts(dd.ap(), dd.ap(), ppc(PP_HM), ALU.mult, [dd, pp], [dd])

    win = dr["w_in"]

    def inproj_xbc(nt, chunks_pieces, premask, postmask, hcol=None, chain=None):
        TB = nt * 128
        W_ = HALO + TB
        prev = None
        for (c0, ch0) in chunks_pieces:
            slot = wpiece([win[:, c0:c0 + 512]])
            for j in range(4):
                wide = nextWide()
                formA(wide, slot, 0, 8, 512, j, uT, W_, W_, [uT])
                conv_p1(wide, ch0 + j, TB, premask, hcol)
                if prev is not None:
                    conv_p2(prev, TB, postmask)
                prev = ch0 + j
        conv_p2(prev, TB, postmask)

    def dt_chain(nt, full, slot, mkcol=None, blocklevel=False, smset=None):
        smset = st["sm"] if smset is None else smset
        raw, m_, na, da, lndt, acum, bias_s, w2, expA, dec, tmp = [b_.sub(0, nt * 32) for b_ in smset[0:11]]
        W_ = HALO + nt * 128
        n32 = nt * 32
        ch = []
        hold = {}

        def v3(b_):
            return b_.ap().rearrange("p (t h) -> p t h", t=nt)

        def s1():
            hold["psd"] = YO.sub(0, 128)
            psd = hold["psd"]
            for i in range(nt):
                for k in range(8):
                    mm(psd, psd.ap(i * 32, (i + 1) * 32), uT.ap(k * W_ + HALO + i * 128, k * W_ + HALO + (i + 1) * 128),
                       slot.ap(k * 32, (k + 1) * 32), k == 0, k == 7, [uT, slot], inc=(k == 7))
        ch.append(s1)
        ch.append(lambda: tt(v3(raw), hold["psd"].ap(0, n32).rearrange("p (t h) -> p t h", t=nt),
                             rows.ap(RW_DTB, RW_DTB + 32).unsqueeze(1).broadcast_to([128, nt, 32]), ALU.add,
                             [hold["psd"], rows], [raw]))
        ch.append(lambda: ts(m_.ap(), raw.ap(), 0.0, ALU.max, [raw], [m_]))
        ch.append(lambda: ts(na.ap(), raw.ap(), -1.0, ALU.mult, [raw], [na]))
        ch.append(lambda: tt(na.ap(), na.ap(), raw.ap(), ALU.min, [na, raw], [na]))
        ch.append(lambda: act(na.ap(), na.ap(), AF.Exp, [na], [na]))
        ch.append(lambda: act(na.ap(), na.ap(), AF.Ln, [na], [na], bias=1.0))
        ch.append(lambda: tt(raw.ap(), m_.ap(), na.ap(), ALU.add, [m_, na], [raw]))
        ch.append(lambda: tt(v3(da), v3(raw), a_bc.ap().unsqueeze(1).broadcast_to([128, nt, 32]), ALU.mult,
                             [raw, a_bc], [da]))
        ch.append(lambda: act(lndt.ap(), raw.ap(), AF.Ln, [raw], [lndt]))

        def s2():
            hold["psa"] = YO.sub(128, 384)
            psa = hold["psa"]
            for i in range(nt):
                mm(psa, psa.ap(i * 32, (i + 1) * 32), tri32.ap(), da.ap(i * 32, (i + 1) * 32), True, True,
                   [tri32, da], inc=True)
                mm(psa, psa.ap(128 + i * 32, 128 + (i + 1) * 32), ones32.ap(), da.ap(i * 32, (i + 1) * 32), True, True,
                   [ones32, da], inc=True)
        ch.append(s2)
        ch.append(lambda: I("dve", lambda e: e.tensor_copy(out=acum.ap(), in_=hold["psa"].ap(0, n32)),
                            [hold["psa"]], [acum]))
        ch.append(lambda: tt(bias_s.ap(), acum.ap(), lndt.ap(), ALU.subtract, [acum, lndt], [bias_s]))
        ch.append(lambda: tt(tmp.ap(), hold["psa"].ap(128, 128 + n32), bias_s.ap(), ALU.subtract,
                             [hold["psa"], bias_s], [tmp]))
        if blocklevel:
            def sfx():
                psa = hold["psa"]
                run = smset[8].sub(0, 32)
                I("dve", lambda e: e.tensor_copy(out=run.ap(), in_=psa.ap(128 + (nt - 1) * 32, 128 + nt * 32)), [psa], [run])
                for i in range(nt - 2, -1, -1):
                    ti = tmp.sub(i * 32, (i + 1) * 32)
                    tt(ti.ap(), ti.ap(), run.ap(), ALU.add, [ti, run], [ti])
                    tt(run.ap(), run.ap(), psa.ap(128 + i * 32, 128 + (i + 1) * 32), ALU.add, [run, psa], [run])
                act(dec.ap(0, 32), run.ap(), AF.Exp, [run], [dec])
            ch.append(sfx)
            ch.append(lambda: act(w2.ap(), tmp.ap(), AF.Exp, [tmp], [w2]))
        else:
            ch.append(lambda: act(w2.ap(), tmp.ap(), AF.Exp, [tmp], [w2]))
            ch.append(lambda: act(dec.ap(), hold["psa"].ap(128, 128 + n32), AF.Exp, [hold["psa"]], [dec]))
        if full:
            ch.append(lambda: act(expA.ap(), acum.ap(), AF.Exp, [acum], [expA]))
        if mkcol is not None:
            ch.append(lambda: ts(w2.ap(), w2.ap(), ppc(mkcol), ALU.mult, [w2, pp], [w2]))
        return ch

    PT3 = BufV(PS, 512, 512)

    def xB_parts(i, mkcol=None, bank=None):
        raw, m_, na, da, lndt, acum, bias_s, w2, expA, dec, tmp = [b_.sub(i * 32, (i + 1) * 32) for b_ in st["sm"][0:11]]
        X = XS[i % 2]
        x_tm, xw, xD, B_tm = X["x_tm"], X["xw"], X["xD"], X["B_tm"]

        def xr(r):
            PT = bank if bank is not None else nextPT()
            for j in range(8):
                ch = r * 8 + j
                tr(PT, PT.ap(j * 128, (j + 1) * 128), xBCc.ap(ch * 512 + i * 128, ch * 512 + (i + 1) * 128),
                   ident_bf.ap(), [xBCc, ident_bf], inc=(j == 7))
            if mkcol is None:
                d = x_tm.sub(r * 1024, (r + 1) * 1024)
                I("act", lambda e, d=d, PT=PT: e.copy(out=d.ap(), in_=PT.ap()), [PT], [d])
            d2 = xw.sub(r * 1024, (r + 1) * 1024)
            tt(d2.ap().rearrange("p (h c) -> p h c", h=16), PT.ap().rearrange("p (h c) -> p h c", h=16),
               w2.ap(r * 16, (r + 1) * 16).unsqueeze(2).broadcast_to([128, 16, 64]), ALU.mult, [PT, w2], [d2])
            d3 = xD.sub(r * 1024, (r + 1) * 1024)
            if mkcol is None:
                tt(d3.ap().rearrange("p (h c) -> p h c", h=16), PT.ap().rearrange("p (h c) -> p h c", h=16),
                   rows.ap(RW_DSK + r * 16, RW_DSK + (r + 1) * 16).unsqueeze(2).broadcast_to([128, 16, 64]),
                   ALU.mult, [PT, rows], [d3])

        def xb_():
            PT = bank if bank is not None else nextPT()
            for j in range(4):
                ch = 16 + j
                tr(PT, PT.ap(j * 128, (j + 1) * 128), xBCc.ap(ch * 512 + i * 128, ch * 512 + (i + 1) * 128),
                   ident_bf.ap(), [xBCc, ident_bf], inc=(j == 3))
            I("act", lambda e, PT=PT: e.copy(out=B_tm.ap(), in_=PT.ap(0, 512)), [PT], [B_tm])
        return [lambda: xr(0), lambda: xr(1), xb_]

    def xB_transposes(i, mkcol=None):
        pump(xB_parts(i, mkcol), 99)

    def state_update(i):
        dec = st["sm"][9].sub(i * 32, (i + 1) * 32)
        B_tm, xw = XS[i % 2]["B_tm"], XS[i % 2]["xw"]
        for g in range(4):
            psd = D0
            mm(psd, psd.ap(), B_tm.ap(g * 128, (g + 1) * 128), xw.ap(g * 512, (g + 1) * 512), True, True,
               [B_tm, xw], inc=True)
            sg = S32.sub(g * 512, (g + 1) * 512)
            tt(sg.ap().rearrange("p (h c) -> p h c", h=8), sg.ap().rearrange("p (h c) -> p h c", h=8),
               dec.ap(g * 8, (g + 1) * 8).unsqueeze(2).broadcast_to([128, 8, 64]), ALU.mult, [sg, dec], [sg])
            tt(sg.ap(), sg.ap(), psd.ap(), ALU.add, [sg, psd], [sg])
            sb = S_bf.sub(g * 512, (g + 1) * 512)
            I("act", lambda e, sb=sb, sg=sg: e.copy(out=sb.ap(), in_=sg.ap()), [sg], [sb])

    SACC = [D0, D1, Qb, ABb.sub(0, 512)]

    def state_accum(i, nt):
        B_tm, xw = XS[i % 2]["B_tm"], XS[i % 2]["xw"]
        for g in range(4):
            mm(SACC[g], SACC[g].ap(), B_tm.ap(g * 128, (g + 1) * 128), xw.ap(g * 512, (g + 1) * 512), i == 0, i == nt - 1,
               [B_tm, xw], inc=True)

    def state_finish():
        dec = st["sm"][9].sub(0, 32)
        for g in range(4):
            sg = S32.sub(g * 512, (g + 1) * 512)
            tt(sg.ap().rearrange("p (h c) -> p h c", h=8), sg.ap().rearrange("p (h c) -> p h c", h=8),
               dec.ap(g * 8, (g + 1) * 8).unsqueeze(2).broadcast_to([128, 8, 64]), ALU.mult, [sg, dec], [sg])
            tt(sg.ap(), sg.ap(), SACC[g].ap(), ALU.add, [sg, SACC[g]], [sg])

    def ssd_y(i, nxt_parts=None):
        raw, m_, na, da, lndt, acum, bias_s, w2, expA, dec, tmp = [b_.sub(i * 32, (i + 1) * 32) for b_ in st["sm"][0:11]]
        c0, c1 = i * 128, (i + 1) * 128
        x_tm, xD = XS[i % 2]["x_tm"], XS[i % 2]["xD"]
        nxt_parts = [] if nxt_parts is None else nxt_parts
        for g in range(4):
            mm(Qb, Qb.ap(g * 128, (g + 1) * 128), xBCc.ap((16 + g) * 512 + c0, (16 + g) * 512 + c1),
               xBCc.ap((20 + g) * 512 + c0, (20 + g) * 512 + c1), True, True, [xBCc], inc=(g == 3))
        tt(CBm.ap().rearrange("p (g c) -> p g c", g=4), Qb.ap().rearrange("p (g c) -> p g c", g=4),
           tri32.ap().unsqueeze(1).broadcast_to([128, 4, 128]), ALU.mult, [Qb, tri32], [CBm])

        def R_(g):
            Rb = Rbs[g % 2]
            tt(Rb.ap().rearrange("p (h c) -> p h c", h=8),
               acum.ap(g * 8, (g + 1) * 8).unsqueeze(2).broadcast_to([128, 8, 128]),
               ident32.ap().unsqueeze(1).broadcast_to([128, 8, 128]), ALU.mult, [acum, ident32], [Rb])

        def AB_(g):
            Rb = Rbs[g % 2]
            for hh in range(2):
                mm(ABb.sub(hh * 512, (hh + 1) * 512), ABb.ap(hh * 512, (hh + 1) * 512), ones32.ap(),
                   Rb.ap(hh * 512, (hh + 1) * 512), True, True, [ones32, Rb], inc=True)

        def SUB_(g):
            Eb = Ebs[g % 2]
            tt(Eb.ap().rearrange("p (h c) -> p h c", h=8), ABb.ap().rearrange("p (h c) -> p h c", h=8),
               bias_s.ap(g * 8, (g + 1) * 8).unsqueeze(2).broadcast_to([128, 8, 128]), ALU.subtract,
               [ABb, bias_s], [Eb])

        def EXP_(g):
            Eb = Ebs[g % 2]
            act(Eb.ap(), Eb.ap(), AF.Exp, [Eb], [Eb])

        def M_(g):
            Eb, Mb = Ebs[g % 2], Mbs[g % 2]
            stt(Mb.ap().rearrange("p (h c) -> p h c", h=8), Eb.ap().rearrange("p (h c) -> p h c", h=8), 1e30,
                CBm.ap(g * 128, (g + 1) * 128).unsqueeze(1).broadcast_to([128, 8, 128]), ALU.min, ALU.mult,
                [Eb, CBm], [Mb])

        def Y_(g):
            Mb = Mbs[g % 2]
            mm(YD, YD.ap(), ident_bf.ap(), xD.ap(g * 512, (g + 1) * 512), True, False, [ident_bf, xD], inc=False)
            for h in range(8):
                hh = g * 8 + h
                mm(YD, YD.ap(h * 64, (h + 1) * 64), Mb.ap(h * 128, (h + 1) * 128), x_tm.ap(hh * 64, (hh + 1) * 64),
                   False, h == 7, [Mb, x_tm], inc=(h == 7))
            mm(YO, YO.ap(), xBCc.ap((20 + g) * 512 + c0, (20 + g) * 512 + c1), S_bf.ap(g * 512, (g + 1) * 512),
               True, True, [xBCc, S_bf], inc=True)

        def T_(g):
            t_ = tb[g % 2]
            tt(t_.ap().rearrange("p (h c) -> p h c", h=8), YO.ap().rearrange("p (h c) -> p h c", h=8),
               expA.ap(g * 8, (g + 1) * 8).unsqueeze(2).broadcast_to([128, 8, 64]), ALU.mult, [YO, expA], [t_])
            tt(t_.ap(), YD.ap(), t_.ap(), ALU.add, [YD, t_], [t_])
            sz = siluz.sub(i * 2048 + g * 512, i * 2048 + (g + 1) * 512)
            tt(t_.ap(), t_.ap(), sz.ap(), ALU.mult, [t_, sz], [t_])

        def N_(g):
            t_ = tb[g % 2]
            sq_ = ssq.sub(4 + (g % 2) * 4, 8 + (g % 2) * 4) if False else ssq.sub(4, 8)
            act(sqj.ap(), t_.ap(), AF.Square, [t_], [sqj, sq_], accum_out=sq_.ap(0, 1))
            act(sq_.ap(1, 2), sq_.ap(0, 1), AF.Ln, [sq_], [sq_], scale=1.0 / 512, bias=EPS)
            act(sq_.ap(2, 3), sq_.ap(1, 2), AF.Exp, [sq_], [sq_], scale=-0.5)

        def YN_(g):
            t_ = tb[g % 2]
            sq_ = ssq.sub(4, 8)
            ts(ynb.ap(), t_.ap(), sq_.ap(2, 3), ALU.mult, [t_, sq_], [ynb])

        def TR_(g):
            for j in range(4):
                tr(PT, PT.ap(j * 128, (j + 1) * 128), ynb.ap(j * 128, (j + 1) * 128), ident_bf.ap(),
                   [ynb, ident_bf], inc=(j == 3))
            for j in range(4):
                ch = g * 4 + j
                d = ynT.sub(ch * 512 + c0, ch * 512 + c1)
                act(d.ap(), PT.ap(j * 128, (j + 1) * 128), AF.Identity, [PT, pp], [d], scale=ppc(PP_GSSD + ch))

        R_(0)
        AB_(0)
        SUB_(0)
        EXP_(0)
        M_(0)
        for g in range(4):
            nx = g + 1 < 4
            if nx:
                R_(g + 1)
            Y_(g)
            if nx:
                AB_(g + 1)
            T_(g)
            if nx:
                SUB_(g + 1)
            N_(g)
            if nx:
                EXP_(g + 1)
            YN_(g)
            if nx:
                M_(g + 1)
            TR_(g)
            pump(nxt_parts, 1)
        pump(nxt_parts, 99)

    I("dve", lambda e: e.memset(S32.ap(), 0.0), [], [S32])
    I("dve", lambda e: e.memset(S_bf.ap(), 0.0), [], [S_bf])
    I("dve", lambda e: e.memset(totsum.ap(), 0.0), [], [totsum])
    I("dve", lambda e: e.memset(vhalo.ap(), 0.0), [], [vhalo])
    XB_PIECES = [(O1 + 512 * q, 4 * q) for q in range(5)]
    ALL_PIECES = [(O1 + 512 * q, 4 * q) for q in range(6)]
    nblk_pre = (NPRE + 3) // 4

    def pre_tiles(blk):
        return [1 + blk * 4 + q for q in range(4) if 1 + blk * 4 + q <= NPRE]

    blocks2 = [[NPRE + 1]] + [[NPRE + 2 + 4 * b + q for q in range(4)] for b in range(4)]
    st["pt2"] = True
    stage_u(pre_tiles(0), HALO + 4 * 128)
    slot0 = wpiece([win[:, O2:O2 + 32]])
    pump(dt_chain(4, False, slot0, mkcol=PP_MK + 0, blocklevel=True, smset=smsets[0]), 99)
    for blk in range(nblk_pre):
        tiles = pre_tiles(blk)
        nt = len(tiles)
        st["W"] = HALO + nt * 128
        st["sm"] = smsets[blk % 2]
        inproj_xbc(nt, XB_PIECES, 0, 0, hcol=PP_MH + blk)
        last = blk + 1 >= nblk_pre
        nxt = blocks2[0] if last else pre_tiles(blk + 1)
        sut = stage_u_thunks(nxt, HALO + len(nxt) * 128)
        slotn = wpiece([win[:, O2:O2 + 32]])
        if last:
            chn = dt_chain(len(nxt), True, slotn, smset=smsets[(blk + 1) % 2])
        else:
            chn = dt_chain(len(nxt), False, slotn, mkcol=PP_MK + blk + 1, blocklevel=True, smset=smsets[(blk + 1) % 2])
        for i in range(nt):
            xB_transposes(i, mkcol=PP_MK + blk)
            state_accum(i, nt)
            pump(sut, 2)
            if not sut:
                pump(chn, 6)
        pump(sut, 99)
        pump(chn, 99)
        state_finish()
        mark("p1b%d" % blk)
    st["pt2"] = False
    I("act", lambda e: e.copy(out=S_bf.ap(), in_=S32.ap()), [S32], [S_bf])
    mark("exch")
    wsso, wpg, wpo, wo, wfi, wfo = (dr["w_ssd_out"], dr["w_pool_grp"], dr["w_pool_out"], dr["w_out"],
                                    dr["w_ffn_in"], dr["w_ffn_out"])
    for bi, tiles in enumerate(blocks2):
        nt = len(tiles)
        TB = nt * 128
        W_ = HALO + TB
        st["W"] = W_
        if bi > 0:
            stage_u(tiles, W_)
        if bi == 0:
            st["sm"] = smsets[nblk_pre % 2]
        else:
            st["sm"] = smsets[0]
            slot = wpiece([win[:, O2:O2 + 32]])
            chain = dt_chain(nt, True, slot)
            pump(chain, 99)
        if bi == 0:
            inproj_xbc(nt, ALL_PIECES, W_, TB)
        else:
            inproj_xbc(nt, ALL_PIECES, HALO if bi == 1 else 0, 0)
        for q in range(4):
            slot = wpiece([win[:, q * 512:(q + 1) * 512]])
            for i in range(nt):
                psd = nextD()
                for k in range(8):
                    mm(psd, psd.ap(), uT.ap(k * W_ + HALO + i * 128, k * W_ + HALO + (i + 1) * 128),
                       slot.ap(k * 512, (k + 1) * 512), k == 0, k == 7, [uT, slot], inc=(k == 7))
                d = siluz.sub(i * 2048 + q * 512, i * 2048 + (q + 1) * 512)
                act(d.ap(), psd.ap(), AF.Silu, [psd], [d])
        xB_transposes(0)
        for i in range(nt):
            ssd_y(i, xB_parts(i + 1, bank=PT3) if i + 1 < nt else None)
            state_update(i)
        mark("ssd%d" % bi)
        for q in range(2):
            slot = wpiece([win[:, O3 + q * 512:O3 + (q + 1) * 512]])
            for j in range(4):
                ch = q * 4 + j
                wide = nextWide()
                formA(wide, slot, 0, 8, 512, j, uT, W_, W_, [uT])
                d = pT.sub(ch * 544, ch * 544 + W_)
                if bi == 1:
                    act(d.ap(0, HALO), wide.ap(0, HALO), AF.Identity, [wide, pp], [d], scale=ppc(PP_HM))
                    act(d.ap(HALO, W_), wide.ap(HALO, W_), AF.Copy, [wide], [d])
                else:
                    act(d.ap(), wide.ap(0, W_), AF.Copy, [wide], [d])
        if True:
            for ch in range(8):
                g = ch // 2
                src = pT.sub(ch * 544, ch * 544 + W_)
                cur = src
                for s_ in range(g + 1):
                    sh = 1 << s_
                    nb = spp[s_ % 2]
                    lo_ = (2 << s_) - 1
                    tt(nb.ap(lo_, W_), cur.ap(lo_, W_), cur.ap(lo_ - sh, W_ - sh), ALU.add, [cur], [nb])
                    cur = nb
                d = pbT.sub(ch * 512, ch * 512 + TB)
                stt(d.ap(), cur.ap(HALO, W_), 1.0 / (1 << (g + 1)), src.ap(HALO, W_), ALU.mult, ALU.subtract,
                    [cur, src], [d])
                if bi == 1:
                    dd = d.sub(0, 32)
                    tt(m1b.ap(0, 32), cur.ap(HALO, HALO + 32), rows.ap(RW_RC + g * 32, RW_RC + (g + 1) * 32), ALU.mult,
                       [cur, rows], [m1b])
                    tt(dd.ap(), m1b.ap(0, 32), src.ap(HALO, HALO + 32), ALU.subtract, [m1b, src], [dd])
        for g in range(4):
            slot = wpiece([wpg[g * 256:(g + 1) * 256, :]])
            for j in range(2):
                ch = g * 2 + j
                psd = nextD()
                for k in range(2):
                    mm(psd, psd.ap(0, TB), slot.ap(k * 256 + j * 128, k * 256 + (j + 1) * 128),
                       pbT.ap((g * 2 + k) * 512, (g * 2 + k) * 512 + TB), k == 0, k == 1, [slot, pbT], inc=(k == 1))
                d = pgT.sub(ch * 512, ch * 512 + TB)
                act(d.ap(), psd.ap(0, TB), AF.Identity, [psd, pp], [d], scale=ppc(PP_PSC + ch))
        for c in range(8):
            slot1 = wpiece([win[:, O4 + c * 128:O4 + (c + 1) * 128], win[:, O4 + 1024 + c * 128:O4 + 1024 + (c + 1) * 128]])
            slot2 = wpiece([wsso[:, c * 128:(c + 1) * 128], wpo[:, c * 128:(c + 1) * 128]])
            for (off, dst) in ((0, gab), (1024, gbb)):
                psd = nextD()
                for k in range(8):
                    mm(psd, psd.ap(0, TB), slot1.ap(off + k * 128, off + (k + 1) * 128),
                       uT.ap(k * W_ + HALO, k * W_ + W_), k == 0, k == 7, [slot1, uT], inc=(k == 7))
                act(dst.ap(0, TB), psd.ap(0, TB), AF.Sigmoid, [psd], [dst])
            psd = nextD()
            for k in range(16):
                mm(psd, psd.ap(0, TB), slot2.ap(k * 128, (k + 1) * 128), ynT.ap(k * 512, k * 512 + TB),
                   k == 0, k == 15, [slot2, ynT], inc=(k == 15))
            tt(m1b.ap(0, TB), psd.ap(0, TB), gab.ap(0, TB), ALU.mult, [psd, gab], [m1b])
            psd = nextD()
            for k in range(8):
                mm(psd, psd.ap(0, TB), slot2.ap(2048 + k * 128, 2048 + (k + 1) * 128), pgT.ap(k * 512, k * 512 + TB),
                   k == 0, k == 7, [slot2, pgT], inc=(k == 7))
            tt(gbb.ap(0, TB), psd.ap(0, TB), gbb.ap(0, TB), ALU.mult, [psd, gbb], [gbb])
            d = mergedT.sub(c * 512, c * 512 + TB)
            tt(d.ap(), m1b.ap(0, TB), gbb.ap(0, TB), ALU.add, [m1b, gbb], [d])
        mark("merge%d" % bi)
        xres = []
        for q in range(2):
            slot = wpiece([wo[:, q * 512:(q + 1) * 512]])
            for i, tl in enumerate(tiles):
                psd = nextD()
                for k in range(8):
                    mm(psd, psd.ap(), mergedT.ap(k * 512 + i * 128, k * 512 + (i + 1) * 128),
                       slot.ap(k * 512, (k + 1) * 512), k == 0, k == 7, [mergedT, slot], inc=(k == 7))
                t_ = tg[i % 2]
                tt(t_.ap(), psd.ap(), g1bc.ap(q * 512, (q + 1) * 512), ALU.mult, [psd, g1bc], [t_])
                hd = hbuf.sub(i * 1024 + q * 512, i * 1024 + (q + 1) * 512)
                xb_ = xt[(i * 2 + q) % 3]
                dma_in(xb_, lambda xb_=xb_: xb_.ap(0, 512), lambda tl=tl, q=q: x_d[tl * 128:(tl + 1) * 128, q * 512:(q + 1) * 512])
                tt(hd.ap(), t_.ap(), xb_.ap(0, 512), ALU.add, [t_, xb_], [hd])
        I("dve", lambda e, W_=W_: e.tensor_copy(out=vT.ap(0, 8 * W_).rearrange("p (k c) -> p k c", k=8)[:, :, 0:HALO],
                                         in_=vhalo.ap().rearrange("p (k c) -> p k c", k=8)), [vhalo], [vT])
        for i in range(nt):
            hb = hbuf.sub(i * 1024, (i + 1) * 1024)
            norm_tile(lambda hb=hb: hb.ap(), hb, 128, vT, HALO + i * 128, A2, SH2, W_)
        I("dve", lambda e, W_=W_, TB=TB: e.tensor_copy(out=vhalo.ap().rearrange("p (k c) -> p k c", k=8),
                                         in_=vT.ap(0, 8 * W_).rearrange("p (k c) -> p k c", k=8)[:, :, TB:TB + HALO]),
          [vT], [vhalo])
        mark("norm2_%d" % bi)
        if bi == 0:
            continue
        pend = None
        fw_i = 0
        for q in range(6):
            nco = min(512, FF - q * 512)
            if pend is not None:
                pend()
                pend = None
            slot = wpiece([wfi[:, FF + q * 512:FF + q * 512 + nco]])
            slotv = wpiece([wfi[:, q * 512:q * 512 + nco]])
            for j in range(nco // 128):
                ch = q * 4 + j
                wide = WIDE[1 + (fw_i % 2)]
                fw_i += 1
                formA(wide, slot, 0, 8, nco, j, vT, W_, W_, [vT])
                if bi == 1:
                    act(wide.ap(0, HALO), wide.ap(0, HALO), AF.Identity, [wide, pp], [wide.sub(0, HALO)], scale=ppc(PP_HM))
                ga_ = gacc[ch % 2]
                act(ga_.ap(0, TB), wide.ap(HALO, W_), AF.Identity, [wide, pp], [ga_],
                    scale=ppc(PP_FW + ch * 3 + 2), bias=ppc(PP_FB + ch))
                for kk in (1, 2):
                    stt(ga_.ap(0, TB), wide.ap(HALO - kk, W_ - kk), ppc(PP_FW + ch * 3 + 2 - kk), ga_.ap(0, TB),
                        ALU.mult, ALU.add, [wide, pp, ga_], [ga_])

                def part2(ch=ch, ga_=ga_, slotv=slotv, nco=nco, j=j):
                    act(ga_.ap(0, TB), ga_.ap(0, TB), AF.Silu, [ga_], [ga_])
                    psd = nextD()
                    for k in range(8):
                        mm(psd, psd.ap(0, TB), slotv.ap(k * nco + j * 128, k * nco + (j + 1) * 128),
                           vT.ap(k * W_ + HALO, k * W_ + W_), k == 0, k == 7, [slotv, vT], inc=(k == 7))
                    d = hiddenT.sub(ch * 512, ch * 512 + TB)
                    tt(d.ap(), psd.ap(0, TB), ga_.ap(0, TB), ALU.mult, [psd, ga_], [d])
                if pend is not None:
                    pend()
                pend = part2
        pend()
        mark("ffnin%d" % bi)
        for q in range(4):
            slA = wpiece([wfo[0:1408, q * 256:(q + 1) * 256]])
            slB = wpiece([wfo[1408:2816, q * 256:(q + 1) * 256]])
            pst = [D0.sub(0, 256), D1.sub(0, 256), Qb.sub(0, 256), YO.sub(0, 256)]
            for (sl, kbase) in ((slA, 0), (slB, 11)):
                for i in range(nt):
                    for k in range(11):
                        kk = kbase + k
                        mm(pst[i], pst[i].ap(), hiddenT.ap(kk * 512 + i * 128, kk * 512 + (i + 1) * 128),
                           sl.ap(k * 256, (k + 1) * 256), kk == 0, kk == 21, [hiddenT, sl], inc=(k == 10))
            for i in range(nt):
                t_ = tg[i % 2]
                tt(t_.ap(0, 256), pst[i].ap(), g2bc.ap(q * 256, (q + 1) * 256), ALU.mult, [pst[i], g2bc], [t_])
                hd = hbuf.sub(i * 1024 + q * 256, i * 1024 + (q + 1) * 256)
                tt(hd.ap(), hd.ap(), t_.ap(0, 256), ALU.add, [hd, t_], [hd])
        for i, tl in enumerate(tiles):
            hb = hbuf.sub(i * 1024, (i + 1) * 1024)
            ob = outb[i % 2]
            act(ob.ap(), hb.ap(), AF.Square, [hb], [ob, ssq], accum_out=ssq.ap(0, 1))
            act(ssq.ap(1, 2), ssq.ap(0, 1), AF.Sqrt, [ssq], [ssq], scale=1.0 / D, bias=EPS)
            I("dve", lambda e: e.reciprocal(out=ssq.ap(2, 3), in_=ssq.ap(1, 2)), [ssq], [ssq])
            stt(ob.ap(), hb.ap(), ssq.ap(2, 3), rows.ap(RW_GFIN, RW_GFIN + 1024), ALU.mult, ALU.mult,
                [hb, ssq, rows], [ob])
            r0 = (tl - NPRE - 2) * 128
            kb.Dm("sp", lambda r0=r0: y_d[r0:r0 + 128, :], lambda ob=ob: ob.ap(), R=[ob], is_out=True)
        mark("blk%d" % bi)


def _host_tables(inp, core):
    f = np.float32
    pp = np.zeros((128, NPP), f)

    def colmajor(v):
        v = np.asarray(v, f).reshape(-1, 128)
        return v.T

    pp[:, PP_C:PP_C + 8] = colmajor(inp["c"][0])
    b = inp["b_ada"][0]
    pp[:, PP_G1:PP_G1 + 8] = colmajor(inp["g_norm1"][0])
    pp[:, PP_BSH1:PP_BSH1 + 8] = colmajor(b[0:1024])
    pp[:, PP_BSC1:PP_BSC1 + 8] = colmajor(b[1024:2048])
    pp[:, PP_G2:PP_G2 + 8] = colmajor(inp["g_norm2"][0])
    pp[:, PP_BSH2:PP_BSH2 + 8] = colmajor(b[3072:4096])
    pp[:, PP_BSC2:PP_BSC2 + 8] = colmajor(b[4096:5120])
    cw = inp["ssd_conv_w"][0]
    for k in range(4):
        pp[:, PP_CW + k:PP_CW + 96:4] = colmajor(cw[k])
    pp[:, PP_CB:PP_CB + 24] = colmajor(inp["ssd_conv_b"][0])
    fw = inp["ffn_conv_w"][0]
    for k in range(3):
        pp[:, PP_FW + k:PP_FW + 66:3] = colmajor(fw[k])
    pp[:, PP_FB:PP_FB + 22] = colmajor(inp["ffn_conv_b"][0])
    pp[:, PP_GSSD:PP_GSSD + 16] = colmajor(inp["g_ssd_norm"][0])
    pp[:, PP_PSC:PP_PSC + 8] = colmajor(inp["pool_scale"][0])
    pp[:, PP_HM] = 0.0 if core == 0 else 1.0
    for j in range(8):
        pp[:, PP_SEL + j] = 1.0 if j < core else 0.0
    for bb in range(28):
        pp[:, PP_MK + bb] = 1.0 if bb >= 28 - 4 * core else 0.0
        pp[:, PP_MH + bb] = 1.0 if (bb - 1) >= 28 - 4 * core else 0.0
    rows = np.zeros((128, NRW), f)
    rows[:, RW_GFIN:RW_GFIN + 1024] = inp["g_final"][None, :]
    rows[:, RW_DTB:RW_DTB + 32] = inp["ssd_dt_bias"][0][None, :]
    rows[:, RW_ALOG:RW_ALOG + 32] = inp["ssd_a_log"][0][None, :]
    rows[:, RW_DSK:RW_DSK + 32] = inp["ssd_d"][0][None, :]
    for g in range(4):
        w = 1 << (g + 1)
        t = np.arange(32) + core * T
        cnt = np.minimum(t + 1, w).astype(f)
        rows[:, RW_RC + g * 32:RW_RC + (g + 1) * 32] = (1.0 / cnt)[None, :]
    bg = np.zeros((128, 2048), f)
    bg[:, 0:1024] = b[2048:3072][None, :]
    bg[:, 1024:2048] = b[5120:6144][None, :]
    return pp, rows, bg


def kernel(**inputs):
    inp = {k: np.asarray(v) for k, v in inputs.items()}
    if "nc" not in _NC_CACHE:
        _NC_CACHE["nc"] = build_program()
    nc = _NC_CACHE["nc"]
    f = np.float32
    x = np.ascontiguousarray(inp["x"][0], dtype=f)
    PADT = 14336 + 128
    xpad = np.concatenate([np.zeros((PADT, D), f), x], axis=0)
    cst = np.zeros((128, 384), f)
    cst[:, 0:128] = np.eye(128, dtype=f)
    cst[:, 128:256] = np.triu(np.ones((128, 128), f))
    cst[:, 256:384] = 1.0
    shared = {
        "w_ada": np.ascontiguousarray(inp["w_ada"][0], f),
        "w_in": np.ascontiguousarray(inp["w_in"][0], f),
        "w_ssd_out": np.ascontiguousarray(inp["w_ssd_out"][0], f),
        "w_pool_grp": np.ascontiguousarray(inp["w_pool_grp"][0].reshape(1024, 256), f),
        "w_pool_out": np.ascontiguousarray(inp["w_pool_out"][0], f),
        "w_out": np.ascontiguousarray(inp["w_out"][0], f),
        "w_ffn_in": np.ascontiguousarray(inp["w_ffn_in"][0], f),
        "w_ffn_out": np.ascontiguousarray(inp["w_ffn_out"][0], f),
        "cst": cst,
    }
    in_maps = []
    for c in range(NCORES):
        pp, rows, bg = _host_tables(inp, c)
        m = dict(shared)
        m["x"] = np.ascontiguousarray(xpad[c * T:c * T + NTS * 128])
        m["pp"] = pp
        m["rows"] = rows
        m["bgate"] = bg
        in_maps.append(m)
    res = run_bass_kernel_spmd(nc, in_maps, core_ids=list(range(NCORES)))
    out = np.concatenate([np.asarray(r["y"]) for r in res.results], axis=0)
    return out.reshape(1, NCORES * T, D).astype(f)
```
